# Optimizing a Trainium2 kernel written in Bass

```python
import math
import jax
import jax.numpy as jnp
from jax import lax
import numpy as np

D_MODEL = 1024
BATCH = 4
SEQ = 8192
DEPTH = 2

CTX_LEN = 256
GRID_W = 64
EPS = 1e-6
ROPE_BASE = 10000.0
ROT_DIM = 64
Q_BLOCK = 128

DA_HEADS = 4
DA_HEAD_DIM = ROT_DIM
DA_V_DIM = 2 * DA_HEAD_DIM
DA_QK_COLS = DA_HEADS * 2 * DA_HEAD_DIM
DA_WIDTH = DA_HEADS * DA_V_DIM

SSD_HEADS = 8
SSD_HEAD_DIM = 64
SSD_INNER = SSD_HEADS * SSD_HEAD_DIM
SSD_GROUPS = 2
SSD_STATE = 128
SSD_CONV = 5
SSD_CHUNK = 128
SSD_XBC = SSD_INNER + 2 * SSD_GROUPS * SSD_STATE

MLA_HEADS = 4
MLA_Q_LORA = 384
MLA_KV_LORA = 256
MLA_NOPE = 128
MLA_ROPE = ROT_DIM
MLA_V = 128
MLA_WIDTH = MLA_HEADS * MLA_V

HY_WIDTH = 512
HY_ORDER = 2
HY_SHORT = 3
HY_EMB = 33
HY_HIDDEN = 64
HY_TARGET = 1e-2
HY_DECAY_SHORT = 0.3
HY_DECAY_LONG = 1.5

N_BRANCH = 4
COLS_DA = 2 * DA_QK_COLS + DA_WIDTH
COLS_SSD = SSD_INNER + SSD_XBC + 2 * SSD_HEADS
COLS_MLA = MLA_Q_LORA + MLA_KV_LORA + MLA_ROPE
COLS_HY = (HY_ORDER + 1) * HY_WIDTH
COLS_GATE = N_BRANCH * D_MODEL
IN_COLS = COLS_DA + COLS_SSD + COLS_MLA + COLS_HY + COLS_GATE

FFN_HIDDEN = -(-8 * D_MODEL // (3 * 256)) * 256

kernel_name = 'hybrid_diffusion_prefix_block'


def rmsnorm(x, g):
    xf = x.astype(jnp.float32)
    y = xf * lax.rsqrt(jnp.mean(xf * xf, axis=-1, keepdims=True) + EPS)
    return (y * g.astype(jnp.float32)).astype(x.dtype)


def modulate(h, shift, scale):
    return h * (1.0 + scale) + shift


def dwconv(u, w, b):
    k, ch = w.shape
    y = lax.conv_general_dilated(u, w[:, None, :].astype(u.dtype), window_strides=(1,),
                                 padding=[(k // 2, k // 2)],
                                 dimension_numbers=('NWC', 'WIO', 'NWC'),
                                 feature_group_count=ch)
    return y + b.astype(u.dtype)


def split_cols(p):
    offs = np.cumsum([COLS_DA, COLS_SSD, COLS_MLA, COLS_HY]).tolist()
    return jnp.split(p, offs, axis=-1)


def axial_rope_tables(rows):
    t = jnp.arange(rows * GRID_W)
    pos_row = (t // GRID_W).astype(jnp.float32)
    pos_col = (t % GRID_W).astype(jnp.float32)
    quarter = ROT_DIM // 4
    inv = ROPE_BASE ** (-jnp.arange(quarter, dtype=jnp.float32) / quarter)
    ang = jnp.concatenate([pos_row[:, None] * inv, pos_col[:, None] * inv], axis=-1)
    return jnp.cos(ang), jnp.sin(ang)


def apply_rope(x, cos, sin):
    shape = (1, x.shape[1]) + (1,) * (x.ndim - 3) + (x.shape[-1] // 2,)
    c = cos.reshape(shape).astype(x.dtype)
    s = sin.reshape(shape).astype(x.dtype)
    x1, x2 = jnp.split(x, 2, axis=-1)
    return jnp.concatenate([x1 * c - x2 * s, x2 * c + x1 * s], axis=-1)


def sweep_query_blocks(fn, *qs):
    b, n = qs[0].shape[:2]
    nb = n // Q_BLOCK
    blocks = tuple(jnp.moveaxis(q.reshape((b, nb, Q_BLOCK) + q.shape[2:]), 1, 0) for q in qs)
    out = lax.map(lambda args: fn(*args), blocks)
    return jnp.moveaxis(out, 0, 1).reshape((b, n) + out.shape[3:])


def diff_attention(q, k, v, lam):
    scale = DA_HEAD_DIM ** -0.5

    def block(qb):
        s = jnp.einsum('bqhcd,bkhcd->bhcqk', qb, k)
        p = jax.nn.softmax(s.astype(jnp.float32) * scale, axis=-1)
        w = (p[:, :, 0] - lam * p[:, :, 1]).astype(v.dtype)
        return jnp.einsum('bhqk,bkhe->bqhe', w, v)

    return sweep_query_blocks(block, q)


def da_project(cols, rope_cs):
    b, n = cols.shape[:2]
    q, k, v = jnp.split(cols, [DA_QK_COLS, 2 * DA_QK_COLS], axis=-1)
    q = q.reshape(b, n, DA_HEADS, 2, DA_HEAD_DIM)
    k = k.reshape(b, n, DA_HEADS, 2, DA_HEAD_DIM)
    v = v.reshape(b, n, DA_HEADS, DA_V_DIM)
    if rope_cs is not None:
        q = apply_rope(q, *rope_cs)
        k = apply_rope(k, *rope_cs)
    return q, k, v


def diff_attn_branch(cols_lat, cols_ctx, lp, lam_init, rope_cs, need_ctx):
    q_l, k_l, v_l = da_project(cols_lat, rope_cs)
    q_c, k_c, v_c = da_project(cols_ctx, None)
    lv = lp['da_lambda'].astype(jnp.float32)
    lam = jnp.exp(jnp.sum(lv[0] * lv[1])) - jnp.exp(jnp.sum(lv[2] * lv[3])) + lam_init

    def finish(o):
        o = rmsnorm(o, lp['da_subln']) * (1.0 - lam_init)
        return o.reshape(o.shape[0], o.shape[1], DA_WIDTH)

    o_lat = finish(diff_attention(q_l, jnp.concatenate([k_c, k_l], axis=1),
                                  jnp.concatenate([v_c, v_l], axis=1), lam))
    o_ctx = finish(diff_attention(q_c, k_c, v_c, lam)) if need_ctx else None
    return o_lat, o_ctx


def segsum(a):
    t = a.shape[-1]
    cs = jnp.cumsum(a, axis=-1)
    d = cs[..., :, None] - cs[..., None, :]
    return jnp.where(jnp.tril(jnp.ones((t, t), dtype=bool)), d, -jnp.inf)


def ssd_scan(xh, dt, a_head, bm, cm, init):
    b, l, h, p = xh.shape
    rep = h // bm.shape[2]
    nc = l // SSD_CHUNK
    bh = jnp.repeat(bm, rep, axis=2).reshape(b, nc, SSD_CHUNK, h, -1)
    ch = jnp.repeat(cm, rep, axis=2).reshape(b, nc, SSD_CHUNK, h, -1)
    xdt = (xh * dt[..., None]).reshape(b, nc, SSD_CHUNK, h, p)
    a = jnp.moveaxis((dt * a_head).reshape(b, nc, SSD_CHUNK, h), 3, 1)
    a_cum = jnp.cumsum(a, axis=-1)
    scores = jnp.einsum('bclhn,bcshn->bhcls', ch, bh) * jnp.exp(segsum(a))
    y_diag = jnp.einsum('bhcls,bcshp->bclhp', scores, xdt)
    decay_to_end = jnp.exp(a_cum[..., -1:] - a_cum)
    chunk_states = jnp.einsum('bclhn,bhcl,bclhp->bchpn', bh, decay_to_end, xdt)
    chunk_decay = jnp.exp(a_cum[..., -1])

    def step(s, inp):
        st, dec = inp
        return s * dec[..., None, None] + st, s

    final, prev = lax.scan(step, init, (jnp.moveaxis(chunk_states, 1, 0), jnp.moveaxis(chunk_decay, 2, 0)))
    y_off = jnp.einsum('bclhn,cbhpn,bhcl->bclhp', ch, prev, jnp.exp(a_cum))
    return (y_diag + y_off).reshape(b, l, h, p), final


def ssd_sequence(xbc, dt_raw, lp, init_f, init_b):
    f32 = jnp.float32
    b, n = xbc.shape[:2]
    xbc = jax.nn.silu(dwconv(xbc, lp['ssd_conv_w'], lp['ssd_conv_b']).astype(f32))
    xs, bm, cm = jnp.split(xbc, [SSD_INNER, SSD_INNER + SSD_GROUPS * SSD_STATE], axis=-1)
    xh = xs.reshape(b, n, SSD_HEADS, SSD_HEAD_DIM)
    bm = bm.reshape(b, n, SSD_GROUPS, SSD_STATE)
    cm = cm.reshape(b, n, SSD_GROUPS, SSD_STATE)
    dt = jax.nn.softplus(dt_raw.astype(f32).reshape(b, n, 2, SSD_HEADS) + lp['ssd_dt_bias'].astype(f32))
    a = -jnp.exp(lp['ssd_a_log'].astype(f32))
    y_f, s_f = ssd_scan(xh, dt[:, :, 0], a[0], bm, cm, init_f)
    rev = lambda t: jnp.flip(t, axis=1)
    y_b, s_b = ssd_scan(rev(xh), rev(dt[:, :, 1]), a[1], rev(bm), rev(cm), init_b)
    y = y_f + rev(y_b) + lp['ssd_d'].astype(f32)[:, None] * xh
    return y.reshape(b, n, SSD_INNER), s_f, s_b


def ssd_branch(cols_lat, cols_ctx, lp, need_ctx):
    z_l, xbc_l, dt_l = jnp.split(cols_lat, [SSD_INNER, SSD_INNER + SSD_XBC], axis=-1)
    z_c, xbc_c, dt_c = jnp.split(cols_ctx, [SSD_INNER, SSD_INNER + SSD_XBC], axis=-1)
    zeros = jnp.zeros((cols_lat.shape[0], SSD_HEADS, SSD_HEAD_DIM, SSD_STATE), jnp.float32)
    y_c, s_f, s_b = ssd_sequence(xbc_c, dt_c, lp, zeros, zeros)
    y_l, _, _ = ssd_sequence(xbc_l, dt_l, lp, s_f, s_b)

    def finish(y, z):
        return rmsnorm(y * jax.nn.silu(z.astype(jnp.float32)), lp['ssd_norm']).astype(z.dtype)

    return finish(y_l, z_l), (finish(y_c, z_c) if need_ctx else None)


def mla_attention(q_nope, q_rope, k_nope, k_rope, v):
    scale = (MLA_NOPE + MLA_ROPE) ** -0.5

    def block(qn, qr):
        s = jnp.einsum('bqhd,bkhd->bhqk', qn, k_nope) + jnp.einsum('bqhr,bkr->bhqk', qr, k_rope)
        p = jax.nn.softmax(s.astype(jnp.float32) * scale, axis=-1).astype(v.dtype)
        return jnp.einsum('bhqk,bkhe->bqhe', p, v)

    return sweep_query_blocks(block, q_nope, q_rope)


def mla_project(cols, lp, rope_cs):
    b, n = cols.shape[:2]
    cq, ckv, kr = jnp.split(cols, [MLA_Q_LORA, MLA_Q_LORA + MLA_KV_LORA], axis=-1)
    q = (rmsnorm(cq, lp['mla_q_norm']) @ lp['mla_w_uq']).reshape(b, n, MLA_HEADS, MLA_NOPE + MLA_ROPE)
    kv = (rmsnorm(ckv, lp['mla_kv_norm']) @ lp['mla_w_ukv']).reshape(b, n, MLA_HEADS, MLA_NOPE + MLA_V)
    q_nope, q_rope = jnp.split(q, [MLA_NOPE], axis=-1)
    k_nope, v = jnp.split(kv, [MLA_NOPE], axis=-1)
    if rope_cs is not None:
        q_rope = apply_rope(q_rope, *rope_cs)
        kr = apply_rope(kr, *rope_cs)
    return q_nope, q_rope, k_nope, kr, v


def mla_branch(cols_lat, cols_ctx, lp, rope_cs, need_ctx):
    qn_l, qr_l, kn_l, kr_l, v_l = mla_project(cols_lat, lp, rope_cs)
    qn_c, qr_c, kn_c, kr_c, v_c = mla_project(cols_ctx, lp, None)
    b = cols_lat.shape[0]
    o_lat = mla_attention(qn_l, qr_l, jnp.concatenate([kn_c, kn_l], axis=1),
                          jnp.concatenate([kr_c, kr_l], axis=1), jnp.concatenate([v_c, v_l], axis=1))
    o_lat = o_lat.reshape(b, -1, MLA_WIDTH)
    o_ctx = mla_attention(qn_c, qr_c, kn_c, kr_c, v_c).reshape(b, -1, MLA_WIDTH) if need_ctx else None
    return o_lat, o_ctx


def hyena_filter_spectrum(n, lp):
    f32 = jnp.float32
    t = jnp.arange(n, dtype=f32)
    t_unit = t / (n - 1)
    bands = (HY_EMB - 1) // 2
    band_f = jnp.linspace(1e-4, bands - 1, bands, dtype=f32)
    w = 2.0 * math.pi * t / n
    feats = jnp.concatenate([t_unit[:, None], jnp.cos(w[:, None] * band_f), -jnp.sin(w[:, None] * band_f)], axis=-1)
    hid = jnp.sin(lp['hy_freq1'].astype(f32) * (feats @ lp['hy_w1'].astype(f32) + lp['hy_b1'].astype(f32)))
    hid = jnp.sin(lp['hy_freq2'].astype(f32) * (hid @ lp['hy_w2'].astype(f32) + lp['hy_b2'].astype(f32)))
    h = (hid @ lp['hy_w3'].astype(f32)).reshape(n, 2, HY_ORDER, HY_WIDTH)
    deltas = jnp.abs(jnp.linspace(math.log(HY_TARGET) / HY_DECAY_LONG, math.log(HY_TARGET) / HY_DECAY_SHORT,
                                  HY_WIDTH, dtype=f32))
    h = h * jnp.exp(-t_unit[:, None] * deltas)[:, None, None, :]
    fwd, bwd = h[:, 0], h[:, 1]
    g = jnp.concatenate([fwd[:1] + bwd[:1], fwd[1:], jnp.zeros_like(fwd[:1]), jnp.flip(bwd[1:], axis=0)], axis=0)
    g = g * lax.rsqrt(jnp.sum(g * g, axis=0, keepdims=True) + EPS)
    return jnp.fft.rfft(g, axis=0)


def fft_conv(u, spec):
    n = u.shape[1]
    y = jnp.fft.irfft(jnp.fft.rfft(u, n=2 * n, axis=1) * spec[None], n=2 * n, axis=1)
    return y[:, :n]


def hyena_sequence(cols, lp):
    n = cols.shape[1]
    u = dwconv(cols, lp['hy_conv_w'], lp['hy_conv_b']).astype(jnp.float32)
    v, x1, x2 = jnp.split(u, HY_ORDER + 1, axis=-1)
    spec = hyena_filter_spectrum(n, lp)
    bias = lp['hy_bias'].astype(jnp.float32)
    z = v
    for o, gate in enumerate((x1, x2)):
        z = gate * (fft_conv(z, spec[:, o]) + bias[o] * z)
    return z.astype(cols.dtype)


def hyena_branch(cols_lat, cols_ctx, lp, need_ctx):
    return hyena_sequence(cols_lat, lp), (hyena_sequence(cols_ctx, lp) if need_ctx else None)


def merge_branches(outs, gate_cols, w_brs, w_out):
    gates = jnp.split(jax.nn.sigmoid(gate_cols), N_BRANCH, axis=-1)
    mixed = sum(g * (o @ w) for g, o, w in zip(gates, outs, w_brs))
    return mixed @ w_out


def token_mixer(h_lat, h_ctx, lp, lam_init, rope_cs, need_ctx):
    da_l, ssd_l, mla_l, hy_l, gate_l = split_cols(h_lat @ lp['w_in'])
    da_c, ssd_c, mla_c, hy_c, gate_c = split_cols(h_ctx @ lp['w_in'])
    branches = (
        diff_attn_branch(da_l, da_c, lp, lam_init, rope_cs, need_ctx),
        ssd_branch(ssd_l, ssd_c, lp, need_ctx),
        mla_branch(mla_l, mla_c, lp, rope_cs, need_ctx),
        hyena_branch(hy_l, hy_c, lp, need_ctx),
    )
    w_brs = (lp['w_br_da'], lp['w_br_ssd'], lp['w_br_mla'], lp['w_br_hy'])
    y_lat = merge_branches([br[0] for br in branches], gate_l, w_brs, lp['w_out'])
    y_ctx = merge_branches([br[1] for br in branches], gate_c, w_brs, lp['w_out']) if need_ctx else None
    return y_lat, y_ctx


def swiglu(h, w1, w3, w2):
    return (jax.nn.silu(h @ w1) * (h @ w3)) @ w2


def setup_inputs(seed: int = 0) -> dict:
    key = jax.random.key(seed)
    keys = iter(jax.random.split(key, 64))
    f32 = jnp.float32
    L = DEPTH

    def normal(shape, scale):
        return scale * jax.random.normal(next(keys), shape, f32)

    def gain(shape):
        return 1.0 + normal(shape, 0.05)

    dt0 = jnp.exp(jax.random.uniform(next(keys), (L, 2, SSD_HEADS), f32, math.log(1e-3), math.log(1e-1)))
    a0 = jax.random.uniform(next(keys), (L, 2, SSD_HEADS), f32, 1.0, 16.0)
    return {
        'x': normal((BATCH, SEQ, D_MODEL), 1.0),
        'c': normal((BATCH, D_MODEL), 1.0),
        'ctx': normal((BATCH, CTX_LEN, D_MODEL), 1.0),
        'c_ctx': normal((D_MODEL,), 1.0),
        'mod_w': normal((L, D_MODEL, 6 * D_MODEL), 0.5 * D_MODEL ** -0.5),
        'mod_b': normal((L, 6 * D_MODEL), 0.02),
        'norm_mix_pre': gain((L, D_MODEL)),
        'norm_mix_post': gain((L, D_MODEL)),
        'norm_ffn_pre': gain((L, D_MODEL)),
        'norm_ffn_post': gain((L, D_MODEL)),
        'w_in': normal((L, D_MODEL, IN_COLS), D_MODEL ** -0.5),
        'da_lambda': normal((L, 4, DA_HEAD_DIM), 0.1),
        'da_subln': gain((L, DA_V_DIM)),
        'ssd_conv_w': normal((L, SSD_CONV, SSD_XBC), SSD_CONV ** -0.5),
        'ssd_conv_b': normal((L, SSD_XBC), 0.02),
        'ssd_a_log': jnp.log(a0),
        'ssd_dt_bias': dt0 + jnp.log(-jnp.expm1(-dt0)),
        'ssd_d': 1.0 + normal((L, SSD_HEADS), 0.1),
        'ssd_norm': gain((L, SSD_INNER)),
        'mla_q_norm': gain((L, MLA_Q_LORA)),
        'mla_w_uq': normal((L, MLA_Q_LORA, MLA_HEADS * (MLA_NOPE + MLA_ROPE)), MLA_Q_LORA ** -0.5),
        'mla_kv_norm': gain((L, MLA_KV_LORA)),
        'mla_w_ukv': normal((L, MLA_KV_LORA, MLA_HEADS * (MLA_NOPE + MLA_V)), MLA_KV_LORA ** -0.5),
        'hy_conv_w': normal((L, HY_SHORT, COLS_HY), HY_SHORT ** -0.5),
        'hy_conv_b': normal((L, COLS_HY), 0.02),
        'hy_w1': normal((L, HY_EMB, HY_HIDDEN), HY_EMB ** -0.5),
        'hy_b1': normal((L, HY_HIDDEN), 0.02),
        'hy_freq1': 1.0 + normal((L, HY_HIDDEN), 0.1),
        'hy_w2': normal((L, HY_HIDDEN, HY_HIDDEN), HY_HIDDEN ** -0.5),
        'hy_b2': normal((L, HY_HIDDEN), 0.02),
        'hy_freq2': 1.0 + normal((L, HY_HIDDEN), 0.1),
        'hy_w3': normal((L, HY_HIDDEN, 2 * HY_ORDER * HY_WIDTH), HY_HIDDEN ** -0.5),
        'hy_bias': normal((L, HY_ORDER, HY_WIDTH), 0.5),
        'w_br_da': normal((L, DA_WIDTH, D_MODEL), DA_WIDTH ** -0.5),
        'w_br_ssd': normal((L, SSD_INNER, D_MODEL), SSD_INNER ** -0.5),
        'w_br_mla': normal((L, MLA_WIDTH, D_MODEL), MLA_WIDTH ** -0.5),
        'w_br_hy': normal((L, HY_WIDTH, D_MODEL), HY_WIDTH ** -0.5),
        'w_out': normal((L, D_MODEL, D_MODEL), D_MODEL ** -0.5),
        'ffn_w1': normal((L, D_MODEL, FFN_HIDDEN), D_MODEL ** -0.5),
        'ffn_w3': normal((L, D_MODEL, FFN_HIDDEN), D_MODEL ** -0.5),
        'ffn_w2': normal((L, FFN_HIDDEN, D_MODEL), FFN_HIDDEN ** -0.5),
    }


def reference(x, c, ctx, c_ctx, mod_w, mod_b, norm_mix_pre, norm_mix_post, norm_ffn_pre,
              norm_ffn_post, w_in, da_lambda, da_subln, ssd_conv_w, ssd_conv_b, ssd_a_log,
              ssd_dt_bias, ssd_d, ssd_norm, mla_q_norm, mla_w_uq, mla_kv_norm, mla_w_ukv,
              hy_conv_w, hy_conv_b, hy_w1, hy_b1, hy_freq1, hy_w2, hy_b2, hy_freq2, hy_w3,
              hy_bias, w_br_da, w_br_ssd, w_br_mla, w_br_hy, w_out, ffn_w1, ffn_w3, ffn_w2):
    n_lat = x.shape[1]
    rows = n_lat // GRID_W
    rope_cs = axial_rope_tables(rows)
    c_act = jax.nn.silu(c)
    cc_act = jax.nn.silu(c_ctx)
    x_lat, x_ctx = x, ctx
    for l in range(DEPTH):
        last = l == DEPTH - 1
        lp = {
            'w_in': w_in[l], 'da_lambda': da_lambda[l], 'da_subln': da_subln[l],
            'ssd_conv_w': ssd_conv_w[l], 'ssd_conv_b': ssd_conv_b[l], 'ssd_a_log': ssd_a_log[l],
            'ssd_dt_bias': ssd_dt_bias[l], 'ssd_d': ssd_d[l], 'ssd_norm': ssd_norm[l],
            'mla_q_norm': mla_q_norm[l], 'mla_w_uq': mla_w_uq[l], 'mla_kv_norm': mla_kv_norm[l],
            'mla_w_ukv': mla_w_ukv[l], 'hy_conv_w': hy_conv_w[l], 'hy_conv_b': hy_conv_b[l],
            'hy_w1': hy_w1[l], 'hy_b1': hy_b1[l], 'hy_freq1': hy_freq1[l], 'hy_w2': hy_w2[l],
            'hy_b2': hy_b2[l], 'hy_freq2': hy_freq2[l], 'hy_w3': hy_w3[l], 'hy_bias': hy_bias[l],
            'w_br_da': w_br_da[l], 'w_br_ssd': w_br_ssd[l], 'w_br_mla': w_br_mla[l],
            'w_br_hy': w_br_hy[l], 'w_out': w_out[l],
        }
        lam_init = 0.8 - 0.6 * math.exp(-0.3 * l)
        mod_lat = jnp.split((c_act @ mod_w[l] + mod_b[l])[:, None, :], 6, axis=-1)
        mod_ctx = jnp.split(cc_act @ mod_w[l] + mod_b[l], 6, axis=-1)
        h_lat = modulate(rmsnorm(x_lat, norm_mix_pre[l]), mod_lat[0], mod_lat[1])
        h_ctx = modulate(rmsnorm(x_ctx, norm_mix_pre[l]), mod_ctx[0], mod_ctx[1])
        y_lat, y_ctx = token_mixer(h_lat, h_ctx, lp, lam_init, rope_cs, not last)
        x_lat = x_lat + mod_lat[2] * rmsnorm(y_lat, norm_mix_post[l])
        f_lat = swiglu(modulate(rmsnorm(x_lat, norm_ffn_pre[l]), mod_lat[3], mod_lat[4]),
                       ffn_w1[l], ffn_w3[l], ffn_w2[l])
        x_lat = x_lat + mod_lat[5] * rmsnorm(f_lat, norm_ffn_post[l])
        if not last:
            x_ctx = x_ctx + mod_ctx[2] * rmsnorm(y_ctx, norm_mix_post[l])
            f_ctx = swiglu(modulate(rmsnorm(x_ctx, norm_ffn_pre[l]), mod_ctx[3], mod_ctx[4]),
                           ffn_w1[l], ffn_w3[l], ffn_w2[l])
            x_ctx = x_ctx + mod_ctx[5] * rmsnorm(f_ctx, norm_ffn_post[l])
    return x_lat
```

```python
import math
import contextlib
import numpy as np
import concourse.bass as bass
import concourse.mybir as mybir
from concourse.bass_utils import run_bass_kernel_spmd

F32 = mybir.dt.float32
BF16 = mybir.dt.bfloat16
I32 = mybir.dt.int32
U8 = mybir.dt.uint8
AF = mybir.ActivationFunctionType
ALU = mybir.AluOpType
AX = mybir.AxisListType

D = 1024
S = 8192
CT = 256
T = S + CT
NL = 2
EPS = 1e-6
FFH = 2816
IN_COLS = 9424
GATE0 = 5328
ARENA = 206 * 1024
TILES = [(0, 256, True)] + [(256 + 512 * i, 512, False) for i in range(16)]
PI = math.pi

COMPUTE = ("pe", "act", "dve", "pool")
ENGINES = ("sync", "pe", "act", "dve", "pool")


class Prog:
    def __init__(self, nc, n_dma_sems=32):
        self.nc = nc
        self.ops = []
        self.last_w = {}
        self.rd_c = {}
        self.rd_d = {}
        self.n_dma_sems = n_dma_sems
        self.barriers = []
        self.strict = False

    def add(self, eng, fn, reads=(), writes=(), dma=False):
        idx = len(self.ops)
        deps = set()
        ex = [k for k in reads if isinstance(k, str) and k.startswith("ps")]
        if ex:
            reads = [k for k in reads if k not in ex]
            writes = list(writes) + [k for k in ex if k not in writes]
        for k in reads:
            w = self.last_w.get(k)
            if w is not None:
                deps.add(w)
        for k in writes:
            w = self.last_w.get(k)
            if w is not None:
                deps.add(w)
            deps.update(self.rd_c.get(k, {}).values())
            deps.update(self.rd_d.get(k, ()))
        for k in reads:
            if dma:
                self.rd_d.setdefault(k, []).append(idx)
            else:
                self.rd_c.setdefault(k, {})[eng] = idx
        for k in writes:
            self.last_w[k] = idx
            self.rd_c[k] = {}
            self.rd_d[k] = []
        deps.discard(idx)
        self.ops.append(dict(eng=eng, fn=fn, deps=deps, dma=dma, sig=dma, strict=self.strict))
        return idx

    def barrier(self):
        self.barriers.append(len(self.ops))
        self.last_w = {}
        self.rd_c = {}
        self.rd_d = {}

    def emit(self):
        nc = self.nc
        ops = self.ops
        bar_pts = sorted(set(self.barriers))
        eng_ops = {e: [] for e in ENGINES}
        for i, o in enumerate(ops):
            eng_ops[o["eng"]].append(i)
        for i, o in enumerate(ops):
            for j in o["deps"]:
                oj = ops[j]
                if not oj["dma"] and (oj["eng"] != o["eng"] or o["strict"]):
                    oj["sig"] = True
        bar_last = []
        for b in bar_pts:
            d = {}
            for e in ENGINES:
                nd = [i for i in eng_ops[e] if i < b and not ops[i]["dma"]]
                if nd:
                    d[e] = nd[-1]
                    ops[nd[-1]]["sig"] = True
            bar_last.append(d)
        cnt = {e: 0 for e in ENGINES}
        dma_k = 0
        dma_cnt = [0] * self.n_dma_sems
        for i, o in enumerate(ops):
            if o["dma"]:
                s = dma_k % self.n_dma_sems
                dma_k += 1
                o["dsem"] = s
                o["dprev"] = dma_cnt[s]
                dma_cnt[s] += 16
                o["dval"] = dma_cnt[s]
            elif o["sig"]:
                cnt[o["eng"]] += 1
                o["val"] = cnt[o["eng"]]
        bar_dma = []
        for b in bar_pts:
            st = [0] * self.n_dma_sems
            for i in range(b):
                o = ops[i]
                if o["dma"]:
                    st[o["dsem"]] = o["dval"]
            bar_dma.append(st)
        self.counts = dict(cnt)
        self.n_ops = len(ops)
        with contextlib.ExitStack() as es:
            esem = {e: es.enter_context(nc.semaphore("es_" + e)) for e in ENGINES}
            dsem = [es.enter_context(nc.semaphore("ds_%d" % k)) for k in range(self.n_dma_sems)]
            block = es.enter_context(nc.Block())

            def run_engine(e, E):
                waited = {}

                def wait(sem, key, val):
                    if waited.get(key, 0) >= val:
                        return
                    waited[key] = val
                    E.wait_ge(sem, val)

                bi = 0
                for i in eng_ops[e]:
                    o = ops[i]
                    while bi < len(bar_pts) and bar_pts[bi] <= i:
                        for e2, j in bar_last[bi].items():
                            if e2 != e:
                                wait(esem[e2], e2, ops[j]["val"])
                        for s, v in enumerate(bar_dma[bi]):
                            if v:
                                wait(dsem[s], ("d", s), v)
                        bi += 1
                    for j in sorted(o["deps"]):
                        oj = ops[j]
                        if oj["dma"]:
                            wait(dsem[oj["dsem"]], ("d", oj["dsem"]), oj["dval"])
                        elif oj["eng"] != e or o["strict"]:
                            wait(esem[oj["eng"]], oj["eng"], oj["val"])
                    if o["dma"] and o["dprev"]:
                        wait(dsem[o["dsem"]], ("d", o["dsem"]), o["dprev"])
                    ins = o["fn"](E)
                    if o["dma"]:
                        ins.then_inc(dsem[o["dsem"]], 16)
                    elif o["sig"]:
                        ins.then_inc(esem[e], 1)
                if e == "sync":
                    for s in range(self.n_dma_sems):
                        if dma_cnt[s]:
                            wait(dsem[s], ("d", s), dma_cnt[s])

            @block.sync
            def _(E):
                run_engine("sync", E)

            @block.tensor
            def _(E):
                run_engine("pe", E)

            @block.scalar
            def _(E):
                run_engine("act", E)

            @block.vector
            def _(E):
                run_engine("dve", E)

            @block.gpsimd
            def _(E):
                run_engine("pool", E)


class Tl:
    def __init__(self, ap, key):
        self.ap = ap
        self.key = key

    def __getitem__(self, idx):
        return self.ap[idx]


class Rot:
    def __init__(self, tiles):
        self.tiles = tiles
        self.i = 0

    def next(self):
        t = self.tiles[self.i % len(self.tiles)]
        self.i += 1
        return t


_ISZ = {F32: 4, BF16: 2, I32: 4}


class KB:
    def __init__(self, dbg=()):
        nc = bass.Bass("TRN2", target_bir_lowering=False)
        self.nc = nc
        self.P = Prog(nc)
        self.dbg = set(dbg)
        self.arena = nc.alloc_sbuf_tensor("arena", [128, ARENA], U8)
        self.off = 0
        self.ps = [Tl(nc.alloc_psum_tensor("ps%d" % i, [128, 512], F32).ap(), "ps%d" % i) for i in range(8)]
        self.ins = {}
        self.uid = 0

    def inp(self, name, shape, dt=F32):
        ap = self.nc.dram_tensor(name, list(shape), dt, kind="ExternalInput").ap()
        self.ins[name] = ap
        return ap

    def dram(self, name, shape, dt=F32, out=False):
        kind = "ExternalOutput" if (out or name in self.dbg) else "Internal"
        return self.nc.dram_tensor(name, list(shape), dt, kind=kind).ap()

    def sb(self, shape, dt=F32, key=None):
        n = int(np.prod(shape[1:])) * _ISZ[dt]
        v = self.arena[0:shape[0], self.off:self.off + n].bitcast(dt)
        if len(shape) == 3:
            v = v.rearrange("p (a b) -> p a b", a=shape[1])
        elif len(shape) == 4:
            v = v.rearrange("p (a b c) -> p a b c", a=shape[1], b=shape[2])
        self.off += (n + 63) // 64 * 64
        assert self.off <= ARENA, "SBUF arena overflow %d" % self.off
        if key is None:
            self.uid += 1
            key = "t%d" % self.uid
        return Tl(v, key)

    def rot(self, n, shape, dt=F32, key="r"):
        self.uid += 1
        return Rot([self.sb(shape, dt, "%s%d_%d" % (key, self.uid, i)) for i in range(n)])

    @staticmethod
    def _k(xs):
        return [x.key if isinstance(x, Tl) else x for x in xs]

    def mm(self, out, lhsT, rhs, start=True, stop=True, r=(), w=()):
        self.P.add("pe", lambda e: e.matmul(out, lhsT=lhsT, rhs=rhs, start=start, stop=stop), self._k(r), self._k(w))

    def tp(self, out, in_, ident, r=(), w=()):
        self.P.add("pe", lambda e: e.transpose(out, in_, ident), self._k(r), self._k(w))

    def act(self, out, in_, func, r=(), w=(), **kw):
        self.P.add("act", lambda e: e.activation(out=out, in_=in_, func=func, **kw), self._k(r), self._k(w))

    def tt(self, out, a, b, op, r=(), w=(), eng="dve"):
        self.P.add(eng, lambda e: e.tensor_tensor(out=out, in0=a, in1=b, op=op), self._k(r), self._k(w))

    def ts(self, out, a, s1, s2, op0, op1=None, r=(), w=(), eng="dve"):
        if op1 is None:
            self.P.add(eng, lambda e: e.tensor_scalar(out=out, in0=a, scalar1=s1, scalar2=None, op0=op0), self._k(r), self._k(w))
        else:
            self.P.add(eng, lambda e: e.tensor_scalar(out=out, in0=a, scalar1=s1, scalar2=s2, op0=op0, op1=op1), self._k(r), self._k(w))

    def stt(self, out, in0, scalar, in1, op0, op1, r=(), w=(), eng="dve"):
        eng = "dve"
        self.P.add(eng, lambda e: e.scalar_tensor_tensor(out=out, in0=in0, scalar=scalar, in1=in1, op0=op0, op1=op1), self._k(r), self._k(w))

    def cp(self, out, in_, r=(), w=(), eng="dve"):
        if eng == "act":
            self.P.add("act", lambda e: e.activation(out=out, in_=in_, func=AF.Copy), self._k(r), self._k(w))
        else:
            self.P.add(eng, lambda e: e.tensor_copy(out=out, in_=in_), self._k(r), self._k(w))

    def ms(self, out, val, w=(), eng="dve"):
        self.P.add(eng, lambda e: e.memset(out, val), [], self._k(w))

    def recip(self, out, in_, r=(), w=()):
        self.P.add("dve", lambda e: e.reciprocal(out=out, in_=in_), self._k(r), self._k(w))

    def ld(self, out, in_, r=(), w=(), **kw):
        self.P.add("sync", lambda e: e.dma_start(out=out, in_=in_, **kw), self._k(r), self._k(w), dma=True)

    def st(self, out, in_, r=(), w=(), **kw):
        self.P.add("pool", lambda e: e.dma_start(out=out, in_=in_, **kw), self._k(r), self._k(w), dma=True)

    def barrier(self):
        self.P.barrier()

    def rsqrt(self, out, in_, r=(), w=()):
        self.act(out, in_, AF.Sqrt, r=list(r) + [self.c_eps], w=w, bias=self.c_eps[:out.shape[0] if hasattr(out, "shape") else 128, 0:1], scale=1.0)
        self.recip(out, out, r=w, w=w)


def fm(ap):
    return ap.rearrange("(c p) t -> p c t", p=128)


def setup_consts(kb, g):
    c = {}
    c["ident"] = kb.sb([128, 128], F32, "ident")
    c["onesd"] = kb.sb([128, 128], F32, "onesd")
    c["ones1"] = kb.sb([128, 128], F32, "ones1")
    c["onesb"] = kb.sb([128, 128], BF16, "onesb")
    c["rrotf"] = kb.sb([128, 128], F32, "rrotf")
    c["rrot"] = kb.sb([128, 128], BF16, "rrot")
    c["eps"] = kb.sb([128, 1], F32, "eps")
    c["MV"] = kb.sb([128, 6, 8, 2], F32, "MV")
    kb.c_eps = c["eps"]
    kb.ld(c["ident"].ap, g["ident"], w=[c["ident"]])
    kb.ld(c["rrotf"].ap, g["rrot"], w=[c["rrotf"]])
    kb.ms(c["onesd"].ap, 1.0 / 1024.0, w=[c["onesd"]])
    kb.ms(c["ones1"].ap, 1.0, w=[c["ones1"]])
    kb.ms(c["eps"].ap, EPS, w=[c["eps"]])
    kb.cp(c["onesb"].ap, c["ones1"].ap, r=[c["ones1"]], w=[c["onesb"]])
    kb.cp(c["rrot"].ap, c["rrotf"].ap, r=[c["rrotf"]], w=[c["rrot"]])
    g["c"] = c


def phase_mod(kb, g, l):
    kb.barrier()
    kb.P.strict = True
    m0 = kb.off
    MV = g["c"]["MV"]
    cT = kb.sb([128, 8, 2])
    ca = kb.sb([128, 8, 2])
    kb.ld(cT.ap, g["cT"].rearrange("p (k j) -> p k j", j=2), w=[cT])
    kb.act(ca.ap, cT.ap, AF.Silu, r=[cT], w=[ca])
    wrot = kb.rot(2, [128, 8, 768])
    psm = kb.ps[0]
    mw = g["mod_w"][l].rearrange("(kc p) n -> p kc n", p=128)
    for gi in range(8):
        wt = wrot.next()
        kb.ld(wt.ap, mw[:, :, gi * 768:(gi + 1) * 768], w=[wt])
        for j in range(6):
            mc = gi * 6 + j
            for kc in range(8):
                kb.mm(psm[:, mc * 2:mc * 2 + 2], wt[:, kc, j * 128:(j + 1) * 128], ca[:, kc, :],
                      start=kc == 0, stop=kc == 7, r=[wt, ca], w=[psm])
    mb = kb.sb([128, 48])
    gn = kb.sb([128, 4, 8])
    kb.ld(mb.ap, g["mod_bT"][l], w=[mb])
    kb.ld(gn.ap, g["gains"][l], w=[gn])
    mod = kb.sb([128, 48, 2])
    psv = psm[:, 0:96].rearrange("p (m j) -> p m j", j=2)
    for j in range(2):
        kb.tt(mod[:, :, j], psv[:, :, j], mb.ap, ALU.add, r=[psm, mb], w=[mod])
    for j in range(2):
        kb.stt(MV[:, 0, :, j], mod[:, 8:16, j], 1.0, gn[:, 0, :], ALU.add, ALU.mult, r=[mod, gn], w=[MV])
        kb.cp(MV[:, 1, :, j], mod[:, 0:8, j], r=[mod], w=[MV])
        kb.tt(MV[:, 2, :, j], mod[:, 16:24, j], gn[:, 1, :], ALU.mult, r=[mod, gn], w=[MV])
        kb.stt(MV[:, 3, :, j], mod[:, 32:40, j], 1.0, gn[:, 2, :], ALU.add, ALU.mult, r=[mod, gn], w=[MV])
        kb.cp(MV[:, 4, :, j], mod[:, 24:32, j], r=[mod], w=[MV])
        kb.tt(MV[:, 5, :, j], mod[:, 40:48, j], gn[:, 3, :], ALU.mult, r=[mod, gn], w=[MV])
    kb.P.strict = False
    kb.off = m0


def norm_mod_tile(kb, g, xt, n, j, tmps, rstd, hT, which, pS):
    c = g["c"]
    MV = c["MV"]
    for kc in range(8):
        kb.act(tmps[kc][:, :n], xt[:, kc, :n], AF.Square, r=[xt], w=[tmps[kc]])
    for kc in range(8):
        kb.mm(pS[:, :n], c["onesd"].ap, tmps[kc][:, :n], start=kc == 0, stop=kc == 7, r=[tmps[kc], c["onesd"]], w=[pS])
    kb.act(rstd[:, :n], pS[:, :n], AF.Sqrt, r=[pS, c["eps"]], w=[rstd], bias=c["eps"][:, 0:1], scale=1.0)
    kb.recip(rstd[:, :n], rstd[:, :n], r=[rstd], w=[rstd])
    for kc in range(8):
        kb.tt(tmps[kc][:, :n], xt[:, kc, :n], rstd[:, :n], ALU.mult, r=[xt, rstd], w=[tmps[kc]])
        kb.act(hT[:, kc, :n], tmps[kc][:, :n], AF.Identity, r=[tmps[kc], MV], w=[hT],
               scale=MV[:, which, kc, j:j + 1], bias=MV[:, which + 1, kc, j:j + 1])


def phase_inproj(kb, g, l, Xin):
    kb.barrier()
    m0 = kb.off
    c = g["c"]
    NW = GATE0
    W = kb.sb([128, 8, NW], BF16, "W1")
    wv = g["w_in"][l].rearrange("(kc p) n -> p kc n", p=128)
    stg = kb.rot(2, [128, 8, 148])
    for pi in range(36):
        s = stg.next()
        kb.ld(s.ap, wv[:, :, pi * 148:(pi + 1) * 148], w=[s])
        kb.cp(W[:, :, pi * 148:(pi + 1) * 148], s.ap, r=[s], w=[W], eng=("dve", "pool")[pi % 2])
    dtb = kb.sb([128, 16])
    kb.ld(dtb.ap, g["ssd_dt_bias"][l].partition_broadcast(128), w=[dtb])
    xrot = kb.rot(2, [128, 8, 512])
    hrot = kb.rot(2, [128, 8, 512], BF16)
    tmps = [kb.sb([128, 512]) for _ in range(8)]
    rstd = kb.sb([128, 512])
    rcrot = kb.rot(2, [128, 512])
    rsrot = kb.rot(2, [128, 512])
    sf = kb.rot(6, [128, 512])
    sbf = kb.rot(4, [128, 512], BF16)
    qbr = kb.rot(2, [128, 512], BF16)
    dts = kb.rot(2, [128, 4, 16])
    psr = Rot(kb.ps[0:5])
    pR = kb.ps[5]
    pD = kb.ps[6]
    pS = kb.ps[7]
    cnt = [0]

    import os
    SKIP = set(os.environ.get("K_SKIP", "").split(","))
    NT = int(os.environ.get("K_NT", "99"))
    for ti, (t0, n, isctx) in enumerate(TILES[:NT]):
        j = 1 if isctx else 0
        xt = xrot.next()
        kb.ld(xt[:, :, :n], fm(Xin)[:, :, t0:t0 + n], w=[xt])
        hT = hrot.next()
        norm_mod_tile(kb, g, xt, n, j, tmps, rstd, hT, 0, pS)
        if "proj" in SKIP:
            continue
        kb.st(fm(g["H"])[:, :, t0:t0 + n], hT[:, :, :n], r=[hT])
        if not isctx and "ropeld" not in SKIP:
            rc = rcrot.next()
            rs = rsrot.next()
            kb.ld(rc[:, :n], g["ropeC"][:, t0 - CT:t0 - CT + n], w=[rc])
            kb.ld(rs[:, :n], g["ropeS"][:, t0 - CT:t0 - CT + n], w=[rs])

        def proj(c0, m):
            pt = psr.next()
            for kc in range(8):
                kb.mm(pt[:m, :n], W[:, kc, c0:c0 + m], hT[:, kc, :n], start=kc == 0, stop=kc == 7, r=[W, hT], w=[pt])
            return pt

        def evac_plain(pt, m, dst_rows, dt=F32, func=None):
            cnt[0] += 1
            s = (sf if dt == F32 else sbf).next()
            if func is not None:
                kb.act(s[:m, :n], pt[:m, :n], func, r=[pt], w=[s])
            elif cnt[0] % 2:
                kb.cp(s[:m, :n], pt[:m, :n], r=[pt], w=[s], eng="act")
            else:
                kb.cp(s[:m, :n], pt[:m, :n], r=[pt], w=[s])
            kb.st(dst_rows[:, t0:t0 + n], s[:m, :n], r=[s])

        def evac_rope(pt, m, dst_rows):
            qb = qbr.next()
            kb.cp(qb[:m, :n], pt[:m, :n], r=[pt], w=[qb], eng="act")
            if "r1" not in SKIP:
                kb.mm(pR[:m, :n], c["rrot"][:m, :m], qb[:m, :n], r=[c["rrot"], qb], w=[pR])
            a = sf.next()
            b = sf.next()
            kb.tt(a[:m, :n], pt[:m, :n], rc[:m, :n], ALU.mult, r=[pt, rc], w=[a])
            if "r2" not in SKIP:
                kb.tt(b[:m, :n], pR[:m, :n], rs[:m, :n], ALU.mult, r=[pR, rs], w=[b])
            else:
                kb.tt(b[:m, :n], a[:m, :n], rs[:m, :n], ALU.mult, r=[a, rs], w=[b])
            o = sbf.next()
            kb.tt(o[:m, :n], a[:m, :n], b[:m, :n], ALU.add, r=[a, b], w=[o])
            kb.st(dst_rows[:, t0:t0 + n], o[:m, :n], r=[o])

        for ch in range(8):
            pt = proj(ch * 128, 128)
            if isctx or "rope" in SKIP:
                evac_plain(pt, 128, g["QK"][ch * 128:(ch + 1) * 128, :], BF16)
            else:
                evac_rope(pt, 128, g["QK"][ch * 128:(ch + 1) * 128, :])
        for i in range(4):
            pt = proj(1536 + i * 128, 128)
            evac_plain(pt, 128, g["ZS"][i * 128:(i + 1) * 128, :], F32, AF.Silu)
        for i in range(8):
            pt = proj(2048 + i * 128, 128)
            evac_plain(pt, 128, g["US"][i * 128:(i + 1) * 128, :])
        for i in range(3):
            pt = proj(3088 + i * 128, 128)
            evac_plain(pt, 128, g["CQ"][i * 128:(i + 1) * 128, :])
        for i in range(2):
            pt = proj(3472 + i * 128, 128)
            evac_plain(pt, 128, g["CKV"][i * 128:(i + 1) * 128, :])
        pt = proj(3728, 64)
        if isctx or "rope" in SKIP:
            evac_plain(pt, 64, g["KR"][0:64, :], BF16)
        else:
            evac_rope(pt, 64, g["KR"][0:64, :])
        for i in range(12):
            pt = proj(3792 + i * 128, 128)
            evac_plain(pt, 128, g["UH"][i * 128:(i + 1) * 128, :])
        nsb = n // 128
        if "tm" in SKIP:
            continue
        for sbk in range(nsb):
            pt = psr.next()
            for kc in range(8):
                kb.mm(pt[:, :512], hT[:, kc, sbk * 128:(sbk + 1) * 128], W[:, kc, 1024:1536],
                      start=kc == 0, stop=kc == 7, r=[W, hT], w=[pt])
            s = sbf.next()
            kb.cp(s[:, :512], pt[:, :512], r=[pt], w=[s], eng=("act", "dve")[sbk % 2])
            kb.st(g["VDA"][t0 + sbk * 128:t0 + (sbk + 1) * 128, :], s[:, :512], r=[s])
        for sbk in range(nsb):
            for kc in range(8):
                kb.mm(pD[:, sbk * 16:(sbk + 1) * 16], hT[:, kc, sbk * 128:(sbk + 1) * 128], W[:, kc, 3072:3088],
                      start=kc == 0, stop=kc == 7, r=[W, hT], w=[pD])
        d = dts.next()
        for sbk in range(nsb):
            kb.tt(d[:, sbk, :], pD[:, sbk * 16:(sbk + 1) * 16], dtb.ap, ALU.add, r=[pD, dtb], w=[d])
        kb.act(d[:, :nsb, :], d[:, :nsb, :], AF.Exp, r=[d], w=[d])
        kb.act(d[:, :nsb, :], d[:, :nsb, :], AF.Ln, r=[d], w=[d], bias=1.0, scale=1.0)
        kb.st(g["DT"][t0:t0 + n, :].rearrange("(s p) h -> p s h", p=128), d[:, :nsb, :], r=[d])
    kb.off = m0


def attn_core(kb, g, KTs, QTs, V, nq_tiles, scale, pS_rot, pO, pZ, prot, nparts):
    c = g["c"]
    for (t0, n, isctx) in nq_tiles:
        nk = 2 if isctx else T // 128
        prev = None

        def flush(pk):
            kt_, pt_ = pk
            kb.mm(pO[:, :n], V[:, kt_, :], pt_[:, :n], start=kt_ == 0, stop=kt_ == nk - 1, r=[V, pt_], w=[pO])
            kb.mm(pZ[:, :n], c["onesb"].ap, pt_[:, :n], start=kt_ == 0, stop=kt_ == nk - 1, r=[c["onesb"], pt_], w=[pZ])

        for kt in range(nk):
            pS = pS_rot.next()
            for pi in range(nparts):
                KT, kk = KTs[pi]
                QT, _ = QTs[pi]
                kb.mm(pS[:, :n], KT[:kk, kt * 128:(kt + 1) * 128], QT[:kk, t0:t0 + n], start=pi == 0, stop=pi == nparts - 1,
                      r=[KT, QT], w=[pS])
            pt = prot.next()
            kb.act(pt[:, :n], pS[:, :n], AF.Exp, r=[pS], w=[pt], scale=scale)
            if prev is not None:
                flush(prev)
            prev = (kt, pt)
        flush(prev)
        yield (t0, n, isctx)


def phase_da(kb, g, l, need_ctx):
    kb.barrier()
    m0 = kb.off
    c = g["c"]
    lam_init = 0.8 - 0.6 * math.exp(-0.3 * l)
    kb.P.strict = True
    lv = kb.sb([128, 4, 64])
    kb.ld(lv.ap, g["da_lambda"][l, 0].partition_broadcast(128).rearrange("p (a d) -> p a d", a=4), w=[lv])
    prod = kb.sb([128, 2, 64])
    kb.tt(prod[:, 0, :], lv[:, 0, :], lv[:, 1, :], ALU.mult, r=[lv], w=[prod])
    kb.tt(prod[:, 1, :], lv[:, 2, :], lv[:, 3, :], ALU.mult, r=[lv], w=[prod])
    sm = kb.sb([128, 2])
    kb.P.add("dve", lambda e: e.reduce_sum(out=sm.ap, in_=prod.ap, axis=AX.X), [prod.key], [sm.key])
    kb.act(sm.ap, sm.ap, AF.Exp, r=[sm], w=[sm])
    nl = kb.sb([128, 1])
    kb.tt(nl.ap, sm[:, 1:2], sm[:, 0:1], ALU.subtract, r=[sm], w=[nl])
    lamc = kb.sb([128, 2])
    kb.ld(lamc.ap, g["lamc"], w=[lamc])
    kb.tt(nl.ap, nl.ap, lamc[:, 0:1], ALU.add, r=[nl, lamc], w=[nl])
    subg = kb.sb([128, 1])
    kb.ld(subg.ap, g["da_subln"][l], w=[subg])
    kb.tt(subg.ap, subg.ap, lamc[:, 1:2], ALU.mult, r=[subg, lamc], w=[subg])
    ones128 = kb.sb([128, 128])
    kb.ms(ones128.ap, 1.0 / 128.0, w=[ones128])
    kb.P.strict = False

    Vr = kb.rot(2, [128, T // 128, 128], BF16)
    KTr = [kb.rot(2, [64, T], BF16) for _ in range(2)]
    QTr = [kb.rot(2, [64, T], BF16) for _ in range(2)]
    prot = kb.rot(3, [128, 512], BF16)
    o1 = kb.sb([128, 512])
    o2 = kb.sb([128, 512])
    rz = kb.sb([128, 512])
    sq = kb.sb([128, 512])
    obr = kb.rot(2, [128, 512], BF16)
    pS_rot = Rot(kb.ps[0:2])
    pO = kb.ps[2:4]
    pZ = kb.ps[4:6]
    pN = kb.ps[6]
    qtiles = [t for t in TILES if (need_ctx or not t[2])]
    for h in range(4):
        V = Vr.next()
        kb.ld(V.ap, g["VDA"][:, h * 128:(h + 1) * 128].rearrange("(kt p) e -> p kt e", p=128), w=[V])
        KT = []
        QT = []
        for cc in range(2):
            k_ = KTr[cc].next()
            q_ = QTr[cc].next()
            kb.ld(k_.ap, g["QK"][512 + h * 128 + cc * 64:512 + h * 128 + cc * 64 + 64, :], w=[k_])
            kb.ld(q_.ap, g["QK"][h * 128 + cc * 64:h * 128 + cc * 64 + 64, :], w=[q_])
            KT.append(k_)
            QT.append(q_)
        gens = [attn_core(kb, g, [(KT[cc], 64)], [(QT[cc], 64)], V, qtiles, 0.125, pS_rot, pO[cc], pZ[cc], prot, 1) for cc in range(2)]
        for _ in qtiles:
            (t0, n, isctx) = next(gens[0])
            next(gens[1])
            kb.recip(rz[:, :n], pZ[0][:, :n], r=[pZ[0]], w=[rz])
            kb.tt(o1[:, :n], pO[0][:, :n], rz[:, :n], ALU.mult, r=[pO[0], rz], w=[o1])
            kb.recip(rz[:, :n], pZ[1][:, :n], r=[pZ[1]], w=[rz])
            kb.tt(o2[:, :n], pO[1][:, :n], rz[:, :n], ALU.mult, r=[pO[1], rz], w=[o2])
            kb.stt(o1[:, :n], o2[:, :n], nl[:, 0:1], o1[:, :n], ALU.mult, ALU.add, r=[o1, o2, nl], w=[o1])
            kb.act(sq[:, :n], o1[:, :n], AF.Square, r=[o1], w=[sq])
            kb.mm(pN[:, :n], ones128.ap, sq[:, :n], r=[ones128, sq], w=[pN])
            kb.act(rz[:, :n], pN[:, :n], AF.Sqrt, r=[pN, c["eps"]], w=[rz], bias=c["eps"][:, 0:1], scale=1.0)
            kb.recip(rz[:, :n], rz[:, :n], r=[rz], w=[rz])
            kb.tt(o1[:, :n], o1[:, :n], rz[:, :n], ALU.mult, r=[o1, rz], w=[o1])
            ob = obr.next()
            kb.ts(ob[:, :n], o1[:, :n], subg[:, 0:1], None, ALU.mult, r=[o1, subg], w=[ob])
            kb.st(g["DAO"][h * 128:(h + 1) * 128, t0:t0 + n], ob[:, :n], r=[ob])
    kb.off = m0


def phase_mla(kb, g, l, need_ctx):
    kb.barrier()
    m0 = kb.off
    c = g["c"]
    Wq = kb.sb([128, 3, 768], BF16, "Wq")
    Wk = kb.sb([128, 2, 1024], BF16, "Wk")
    st1 = kb.sb([128, 3, 768])
    st2 = kb.sb([128, 2, 1024])
    kb.ld(st1.ap, g["mla_w_uq"][l].rearrange("(kc p) n -> p kc n", p=128), w=[st1])
    kb.ld(st2.ap, g["mla_w_ukv"][l].rearrange("(kc p) n -> p kc n", p=128), w=[st2])
    kb.cp(Wq.ap, st1.ap, r=[st1], w=[Wq])
    kb.cp(Wk.ap, st2.ap, r=[st2], w=[Wk], eng="pool")
    gq = kb.sb([128, 3])
    gk = kb.sb([128, 2])
    kb.ld(gq.ap, g["mla_q_norm"][l], w=[gq])
    kb.ld(gk.ap, g["mla_kv_norm"][l], w=[gk])
    on384 = kb.sb([128, 128])
    on256 = kb.sb([128, 128])
    kb.ms(on384.ap, 1.0 / 384.0, w=[on384])
    kb.ms(on256.ap, 1.0 / 256.0, w=[on256])
    m1 = kb.off
    xr = kb.rot(2, [128, 5, 512])
    sqs = [kb.sb([128, 512]) for _ in range(5)]
    rstd = kb.sb([128, 512])
    nb = kb.rot(2, [128, 5, 512], BF16)
    rcrot = kb.rot(2, [64, 512])
    rsrot = kb.rot(2, [64, 512])
    sf = kb.rot(4, [128, 512])
    sbf = kb.rot(4, [128, 512], BF16)
    qbr = kb.rot(2, [64, 512], BF16)
    psr = Rot(kb.ps[0:5])
    pR = kb.ps[5]
    pS = kb.ps[7]
    cnt = 0
    for (t0, n, isctx) in TILES:
        xt = xr.next()
        kb.ld(xt[:, 0:3, :n], fm(g["CQ"])[:, :, t0:t0 + n], w=[xt])
        kb.ld(xt[:, 3:5, :n], fm(g["CKV"])[:, :, t0:t0 + n], w=[xt])
        if not isctx:
            rc = rcrot.next()
            rs = rsrot.next()
            kb.ld(rc[:, :n], g["ropeC"][0:64, t0 - CT:t0 - CT + n], w=[rc])
            kb.ld(rs[:, :n], g["ropeS"][0:64, t0 - CT:t0 - CT + n], w=[rs])
        xn = nb.next()
        for (c0, c1, onesm, gg) in [(0, 3, on384, gq), (3, 5, on256, gk)]:
            for kc in range(c0, c1):
                kb.act(sqs[kc][:, :n], xt[:, kc, :n], AF.Square, r=[xt], w=[sqs[kc]])
            for kc in range(c0, c1):
                kb.mm(pS[:, :n], onesm.ap, sqs[kc][:, :n], start=kc == c0, stop=kc == c1 - 1, r=[onesm, sqs[kc]], w=[pS])
            kb.act(rstd[:, :n], pS[:, :n], AF.Sqrt, r=[pS, c["eps"]], w=[rstd], bias=c["eps"][:, 0:1], scale=1.0)
            kb.recip(rstd[:, :n], rstd[:, :n], r=[rstd], w=[rstd])
            for kc in range(c0, c1):
                kb.tt(sqs[kc][:, :n], xt[:, kc, :n], rstd[:, :n], ALU.mult, r=[xt, rstd], w=[sqs[kc]])
                kb.act(xn[:, kc, :n], sqs[kc][:, :n], AF.Copy, r=[sqs[kc], gg], w=[xn], scale=gg[:, kc - c0:kc - c0 + 1])

        def evac(pt, m, dst):
            nonlocal cnt
            cnt += 1
            s_ = sbf.next()
            kb.cp(s_[:m, :n], pt[:m, :n], r=[pt], w=[s_], eng=("act", "dve")[cnt % 2])
            kb.st(dst[:, t0:t0 + n], s_[:m, :n], r=[s_])

        for h in range(4):
            pt = psr.next()
            for kc in range(3):
                kb.mm(pt[:, :n], Wq[:, kc, h * 192:h * 192 + 128], xn[:, kc, :n], start=kc == 0, stop=kc == 2, r=[Wq, xn], w=[pt])
            evac(pt, 128, g["QN"][h * 128:(h + 1) * 128, :])
            pt = psr.next()
            for kc in range(3):
                kb.mm(pt[:64, :n], Wq[:, kc, h * 192 + 128:h * 192 + 192], xn[:, kc, :n], start=kc == 0, stop=kc == 2, r=[Wq, xn], w=[pt])
            if isctx:
                evac(pt, 64, g["QR"][h * 64:(h + 1) * 64, :])
            else:
                qb = qbr.next()
                kb.cp(qb[:, :n], pt[:64, :n], r=[pt], w=[qb], eng="act")
                kb.mm(pR[:64, :n], c["rrot"][:64, :64], qb[:, :n], r=[c["rrot"], qb], w=[pR])
                a = sf.next()
                b = sf.next()
                kb.tt(a[:64, :n], pt[:64, :n], rc[:, :n], ALU.mult, r=[pt, rc], w=[a])
                kb.tt(b[:64, :n], pR[:64, :n], rs[:, :n], ALU.mult, r=[pR, rs], w=[b])
                o = sbf.next()
                kb.tt(o[:64, :n], a[:64, :n], b[:64, :n], ALU.add, r=[a, b], w=[o])
                kb.st(g["QR"][h * 64:(h + 1) * 64, t0:t0 + n], o[:64, :n], r=[o])
            pt = psr.next()
            for kc in range(2):
                kb.mm(pt[:, :n], Wk[:, kc, h * 256:h * 256 + 128], xn[:, 3 + kc, :n], start=kc == 0, stop=kc == 1, r=[Wk, xn], w=[pt])
            evac(pt, 128, g["KN"][h * 128:(h + 1) * 128, :])
        for sbk in range(n // 128):
            pt = psr.next()
            for h in range(4):
                for kc in range(2):
                    kb.mm(pt[:, h * 128:(h + 1) * 128], xn[:, 3 + kc, sbk * 128:(sbk + 1) * 128], Wk[:, kc, h * 256 + 128:h * 256 + 256],
                          start=kc == 0, stop=kc == 1, r=[Wk, xn], w=[pt])
            s_ = sbf.next()
            kb.cp(s_[:, :512], pt[:, :512], r=[pt], w=[s_], eng=("act", "dve")[sbk % 2])
            kb.st(g["VM"][t0 + sbk * 128:t0 + (sbk + 1) * 128, :], s_[:, :512], r=[s_])
    kb.barrier()
    kb.off = m1
    KR = kb.sb([64, T], BF16)
    kb.ld(KR.ap, g["KR"], w=[KR])
    Vr = kb.rot(2, [128, T // 128, 128], BF16)
    KNr = kb.rot(2, [128, T], BF16)
    QNr = kb.rot(2, [128, T], BF16)
    QRr = kb.rot(2, [64, T], BF16)
    prot = kb.rot(3, [128, 512], BF16)
    rz = kb.sb([128, 512])
    obr = kb.rot(2, [128, 512], BF16)
    pS_rot = Rot(kb.ps[0:2])
    pO = kb.ps[2]
    pZ = kb.ps[4]
    qtiles = [t for t in TILES if (need_ctx or not t[2])]
    scale = 192.0 ** -0.5
    for h in range(4):
        V = Vr.next()
        kb.ld(V.ap, g["VM"][:, h * 128:(h + 1) * 128].rearrange("(kt p) e -> p kt e", p=128), w=[V])
        KN = KNr.next()
        QN = QNr.next()
        QR = QRr.next()
        kb.ld(KN.ap, g["KN"][h * 128:(h + 1) * 128, :], w=[KN])
        kb.ld(QN.ap, g["QN"][h * 128:(h + 1) * 128, :], w=[QN])
        kb.ld(QR.ap, g["QR"][h * 64:(h + 1) * 64, :], w=[QR])
        for (t0, n, isctx) in attn_core(kb, g, [(KN, 128), (KR, 64)], [(QN, 128), (QR, 64)], V, qtiles, scale, pS_rot, pO, pZ, prot, 2):
            kb.recip(rz[:, :n], pZ[:, :n], r=[pZ], w=[rz])
            ob = obr.next()
            kb.tt(ob[:, :n], pO[:, :n], rz[:, :n], ALU.mult, r=[pO, rz], w=[ob])
            kb.st(g["MLAO"][h * 128:(h + 1) * 128, t0:t0 + n], ob[:, :n], r=[ob])
    kb.off = m0


def cast_load(kb, dst, src_view, ncols, piece, stg, engs=("dve", "pool")):
    i = 0
    for c0 in range(0, ncols, piece):
        c1 = min(ncols, c0 + piece)
        s_ = stg.next()
        kb.ld(s_[:, :, :c1 - c0], src_view[:, :, c0:c1], w=[s_])
        kb.cp(dst[:, :, c0:c1], s_[:, :, :c1 - c0], r=[s_], w=[dst], eng=engs[i % len(engs)])
        i += 1


def post_norm_residual(kb, g, y, xt, n, j, which, sqs, rstd, pS):
    c = g["c"]
    MV = c["MV"]
    for kc in range(8):
        kb.act(sqs[kc][:, :n], y[:, kc, :n], AF.Square, r=[y], w=[sqs[kc]])
    for kc in range(8):
        kb.mm(pS[:, :n], c["onesd"].ap, sqs[kc][:, :n], start=kc == 0, stop=kc == 7, r=[sqs[kc], c["onesd"]], w=[pS])
    kb.act(rstd[:, :n], pS[:, :n], AF.Sqrt, r=[pS, c["eps"]], w=[rstd], bias=c["eps"][:, 0:1], scale=1.0)
    kb.recip(rstd[:, :n], rstd[:, :n], r=[rstd], w=[rstd])
    for kc in range(8):
        kb.tt(sqs[kc][:, :n], y[:, kc, :n], rstd[:, :n], ALU.mult, r=[y, rstd], w=[sqs[kc]])
        kb.stt(xt[:, kc, :n], sqs[kc][:, :n], MV[:, which, kc, j:j + 1], xt[:, kc, :n], ALU.mult, ALU.add,
               r=[sqs[kc], MV, xt], w=[xt], eng=("dve", "pool")[kc % 2])


def phase_merge(kb, g, l, Xin, Xout, need_ctx):
    kb.barrier()
    m0 = kb.off
    c = g["c"]
    Wg = kb.sb([128, 8, 4096], BF16, "Wg")
    Wb = kb.sb([128, 16, 1024], BF16, "Wb")
    Wo = kb.sb([128, 8, 1024], BF16, "Wo")
    stg = kb.rot(2, [128, 8, 256])
    cast_load(kb, Wg, g["w_in"][l].rearrange("(kc p) n -> p kc n", p=128)[:, :, GATE0:IN_COLS], 4096, 256, stg)
    for bi, nm in enumerate(["w_br_da", "w_br_ssd", "w_br_mla", "w_br_hy"]):
        cast_load(kb, Tl(Wb[:, bi * 4:(bi + 1) * 4, :], Wb.key), g[nm][l].rearrange("(kc p) n -> p kc n", p=128), 1024, 256,
                  Rot([Tl(t.ap[:, 0:4, :], t.key) for t in stg.tiles]))
    cast_load(kb, Wo, g["w_out"][l].rearrange("(kc p) n -> p kc n", p=128), 1024, 256, stg)
    NT = 256
    hrot = kb.rot(2, [128, 8, NT], BF16)
    brot = kb.rot(2, [128, 16, NT], BF16)
    xrot = kb.rot(2, [128, 8, NT])
    mixed = kb.sb([128, 8, NT], BF16)
    y = kb.sb([128, 8, NT])
    sig = kb.rot(2, [128, NT])
    acc = kb.sb([128, NT])
    tmp = kb.rot(2, [128, NT])
    sqs = [kb.sb([128, NT]) for _ in range(8)]
    rstd = kb.sb([128, NT])
    pG = Rot(kb.ps[0:3])
    pB = Rot(kb.ps[3:6])
    pY = kb.ps[6]
    pS = kb.ps[7]
    srcs = ["DAO", "SSDO", "MLAO", "HYO"]
    for t0 in range(0, T, NT):
        n = NT
        isctx = t0 < CT
        if isctx and not need_ctx:
            continue
        j = 1 if isctx else 0
        hT = hrot.next()
        kb.ld(hT[:, :, :n], fm(g["H"])[:, :, t0:t0 + n], w=[hT])
        bt = brot.next()
        for bi, nm in enumerate(srcs):
            kb.ld(bt[:, bi * 4:(bi + 1) * 4, :n], fm(g[nm])[:, :, t0:t0 + n], w=[bt])
        xt = xrot.next()
        kb.ld(xt[:, :, :n], fm(Xin)[:, :, t0:t0 + n], w=[xt])
        for fc in range(8):
            for bi in range(4):
                pg = pG.next()
                for kc in range(8):
                    kb.mm(pg[:, :n], Wg[:, kc, bi * 1024 + fc * 128:bi * 1024 + (fc + 1) * 128], hT[:, kc, :n],
                          start=kc == 0, stop=kc == 7, r=[Wg, hT], w=[pg])
                sg = sig.next()
                kb.act(sg[:, :n], pg[:, :n], AF.Sigmoid, r=[pg], w=[sg])
                pb = pB.next()
                for kc in range(4):
                    kb.mm(pb[:, :n], Wb[:, bi * 4 + kc, fc * 128:(fc + 1) * 128], bt[:, bi * 4 + kc, :n],
                          start=kc == 0, stop=kc == 3, r=[Wb, bt], w=[pb])
                if bi == 0:
                    kb.tt(acc[:, :n], pb[:, :n], sg[:, :n], ALU.mult, r=[pb, sg], w=[acc])
                else:
                    tp_ = tmp.next()
                    kb.tt(tp_[:, :n], pb[:, :n], sg[:, :n], ALU.mult, r=[pb, sg], w=[tp_])
                    if bi < 3:
                        kb.tt(acc[:, :n], acc[:, :n], tp_[:, :n], ALU.add, r=[acc, tp_], w=[acc], eng="pool")
                    else:
                        kb.tt(mixed[:, fc, :n], acc[:, :n], tp_[:, :n], ALU.add, r=[acc, tp_], w=[mixed], eng="pool")
        for fc in range(8):
            for kc in range(8):
                kb.mm(pY[:, :n], Wo[:, kc, fc * 128:(fc + 1) * 128], mixed[:, kc, :n], start=kc == 0, stop=kc == 7, r=[Wo, mixed], w=[pY])
            kb.cp(y[:, fc, :n], pY[:, :n], r=[pY], w=[y], eng="act")
        post_norm_residual(kb, g, y, xt, n, j, 2, sqs, rstd, pS)
        kb.st(fm(Xout)[:, :, t0:t0 + n], xt[:, :, :n], r=[xt])
    kb.off = m0


def phase_ffn(kb, g, l, Xin, Xout, need_ctx, Yout=None):
    kb.barrier()
    m0 = kb.off
    c = g["c"]
    NH = FFH // 128
    W1 = kb.sb([128, 8, FFH], BF16, "W1f")
    W3 = kb.sb([128, 8, FFH], BF16, "W3f")
    W2 = kb.sb([128, NH, 1024], BF16, "W2f")
    stg = kb.rot(2, [128, 8, 256])
    cast_load(kb, W1, g["ffn_w1"][l].rearrange("(kc p) n -> p kc n", p=128), FFH, 256, stg)
    cast_load(kb, W3, g["ffn_w3"][l].rearrange("(kc p) n -> p kc n", p=128), FFH, 256, stg)
    w2v = g["ffn_w2"][l].rearrange("(kc p) n -> p kc n", p=128)
    for q in range(0, NH, 2):
        s_ = stg.next()
        sv = s_.ap.rearrange("p a b -> p (a b)").rearrange("p (a b) -> p a b", a=2)
        kb.ld(sv, w2v[:, q:q + 2, :], w=[s_])
        kb.cp(W2[:, q:q + 2, :], sv, r=[s_], w=[W2], eng=("dve", "pool")[(q // 2) % 2])
    NT = 256
    xrot = kb.rot(2, [128, 8, NT])
    hT = kb.sb([128, 8, NT], BF16)
    a_t = kb.sb([128, NH, NT], BF16)
    f = kb.sb([128, 8, NT])
    s1 = kb.rot(2, [128, NT])
    sqs = [kb.sb([128, NT]) for _ in range(8)]
    rstd = kb.sb([128, NT])
    p1 = Rot(kb.ps[0:2])
    p3 = Rot(kb.ps[2:4])
    pF = Rot(kb.ps[4:6])
    pS = kb.ps[7]
    for t0 in range(0, T, NT):
        n = NT
        isctx = t0 < CT
        if isctx and not need_ctx:
            continue
        j = 1 if isctx else 0
        xt = xrot.next()
        kb.ld(xt[:, :, :n], fm(Xin)[:, :, t0:t0 + n], w=[xt])
        norm_mod_tile(kb, g, xt, n, j, sqs, rstd, hT, 3, pS)
        for hc in range(NH):
            a1 = p1.next()
            a3 = p3.next()
            for kc in range(8):
                kb.mm(a1[:, :n], W1[:, kc, hc * 128:(hc + 1) * 128], hT[:, kc, :n], start=kc == 0, stop=kc == 7, r=[W1, hT], w=[a1])
            for kc in range(8):
                kb.mm(a3[:, :n], W3[:, kc, hc * 128:(hc + 1) * 128], hT[:, kc, :n], start=kc == 0, stop=kc == 7, r=[W3, hT], w=[a3])
            s_ = s1.next()
            kb.act(s_[:, :n], a1[:, :n], AF.Silu, r=[a1], w=[s_])
            kb.tt(a_t[:, hc, :n], a3[:, :n], s_[:, :n], ALU.mult, r=[a3, s_], w=[a_t])
        for fc in range(8):
            pf = pF.next()
            for hc in range(NH):
                kb.mm(pf[:, :n], W2[:, hc, fc * 128:(fc + 1) * 128], a_t[:, hc, :n], start=hc == 0, stop=hc == NH - 1, r=[W2, a_t], w=[pf])
            kb.cp(f[:, fc, :n], pf[:, :n], r=[pf], w=[f], eng="act")
        post_norm_residual(kb, g, f, xt, n, j, 5, sqs, rstd, pS)
        if Yout is not None:
            kb.st(fm(Yout)[:, :, t0 - CT:t0 - CT + n], xt[:, :, :n], r=[xt])
        else:
            kb.st(fm(Xout)[:, :, t0:t0 + n], xt[:, :, :n], r=[xt])
    kb.off = m0


def seq_bounds(t0, n):
    return (0, CT) if t0 < CT else (CT, T)


def phase_ssd(kb, g, l, need_ctx):
    c = g["c"]
    NB = T // 128
    kb.barrier()
    m0 = kb.off
    cw = kb.sb([128, 8, 5])
    cb = kb.sb([128, 8])
    kb.ld(cw.ap, g["ssd_conv_w"][l], w=[cw])
    kb.ld(cb.ap, g["ssd_conv_b"][l], w=[cb])
    ubr = kb.rot(3, [128, 516])
    accr = kb.rot(2, [128, 512])
    outr = kb.rot(3, [128, 512])
    outb = kb.rot(2, [128, 512], BF16)
    tmr = kb.rot(2, [128, 512])
    pT = Rot(kb.ps[0:4])
    for (t0, n, isctx) in TILES:
        lo, hi = seq_bounds(t0, n)
        a0 = max(lo, t0 - 2)
        a1 = min(hi, t0 + n + 2)
        for kc in range(8):
            ub = ubr.next()
            if a0 > t0 - 2 or a1 < t0 + n + 2:
                kb.ms(ub[:, 0:n + 4], 0.0, w=[ub], eng="pool")
            kb.ld(ub[:, a0 - (t0 - 2):a1 - (t0 - 2)], g["US"][kc * 128:(kc + 1) * 128, a0:a1], w=[ub])
            acc = accr.next()
            kb.ts(acc[:, :n], ub[:, 0:n], cw[:, kc, 0:1], None, ALU.mult, r=[ub, cw], w=[acc])
            for j in range(1, 5):
                kb.stt(acc[:, :n], ub[:, j:j + n], cw[:, kc, j:j + 1], acc[:, :n], ALU.mult, ALU.add, r=[ub, cw, acc], w=[acc])
            if kc < 4:
                o = outr.next()
                kb.act(o[:, :n], acc[:, :n], AF.Silu, r=[acc, cb], w=[o], bias=cb[:, kc:kc + 1], scale=1.0)
                kb.st(g["XSF"][kc * 128:(kc + 1) * 128, t0:t0 + n], o[:, :n], r=[o])
                tm = tmr.next()
                for sbk in range(n // 128):
                    pt = pT.next()
                    kb.tp(pt[:, 0:128], o[:, sbk * 128:(sbk + 1) * 128], c["ident"].ap, r=[o, c["ident"]], w=[pt])
                    kb.cp(tm[:, sbk * 128:(sbk + 1) * 128], pt[:, 0:128], r=[pt], w=[tm], eng=("dve", "act")[sbk % 2])
                kb.st(g["XSTM"][t0:t0 + n, kc * 128:(kc + 1) * 128].rearrange("(s p) e -> p s e", p=128),
                      tm[:, :n].rearrange("p (s e) -> p s e", e=128), r=[tm])
            else:
                o = outb.next()
                kb.act(o[:, :n], acc[:, :n], AF.Silu, r=[acc, cb], w=[o], bias=cb[:, kc:kc + 1], scale=1.0)
                kb.st(g["XBC"][(kc - 4) * 128:(kc - 3) * 128, t0:t0 + n], o[:, :n], r=[o])
    kb.barrier()
    kb.off = m0
    DTt = kb.sb([128, NB, 16], F32, "DTt")
    kb.ld(DTt.ap, g["DT"].rearrange("(b p) h -> p b h", p=128), w=[DTt])
    NCS = kb.sb([128, NB, 16], F32, "NCS")
    m1 = kb.off
    Arow = kb.sb([128, 16])
    kb.ld(Arow.ap, g["ssd_a_log"][l, 0].partition_broadcast(128), w=[Arow])
    kb.act(Arow.ap, Arow.ap, AF.Exp, r=[Arow], w=[Arow])
    kb.ts(Arow.ap, Arow.ap, -1.0, None, ALU.mult, r=[Arow], w=[Arow])
    tri = kb.sb([128, 2, 128], F32, "tri")
    kb.ld(tri.ap, g["tri"], w=[tri])
    Atm = kb.sb([128, NB, 16], F32, "Atm")
    for b in range(NB):
        kb.tt(Atm[:, b, :], DTt[:, b, :], Arow.ap, ALU.mult, r=[DTt, Arow], w=[Atm])
    CS = [kb.sb([8, T], F32, "CSF"), kb.sb([8, T], F32, "CSB")]
    carry = [kb.sb([8, 1]), kb.sb([8, 1])]
    ncar = [kb.sb([128, 8]), kb.sb([128, 8])]
    for d in range(2):
        kb.ms(carry[d].ap, 0.0, w=[carry[d]])
        kb.ms(ncar[d].ap, 0.0, w=[ncar[d]])
    order_f = list(range(NB))
    order_b = [1, 0] + list(range(NB - 1, 1, -1))
    kb.P.strict = True
    pA = Rot(kb.ps[0:2])
    pBm = Rot(kb.ps[2:4])
    pC = Rot(kb.ps[4:6])
    for d, order in enumerate([order_f, order_b]):
        for b in order:
            cols = slice(d * 8, d * 8 + 8)
            p1 = pA.next()
            kb.mm(p1[:8, 0:128], Atm[:, b, cols], tri[:, d, :], r=[Atm, tri], w=[p1])
            kb.ts(CS[d][:, b * 128:(b + 1) * 128], p1[:8, 0:128], carry[d][:, 0:1], None, ALU.add, r=[p1, carry[d]], w=[CS[d]])
            edge = (b * 128 + 127) if d == 0 else (b * 128)
            kb.cp(carry[d].ap, CS[d][:, edge:edge + 1], r=[CS[d]], w=[carry[d]])
            p2 = pBm.next()
            kb.mm(p2[:, 0:8], tri[:, d, :], Atm[:, b, cols], r=[Atm, tri], w=[p2])
            kb.stt(NCS[:, b, cols], p2[:, 0:8], -1.0, ncar[d].ap, ALU.mult, ALU.add, r=[p2, ncar[d]], w=[NCS])
            p3 = pC.next()
            kb.mm(p3[:, 0:8], c["ones1"].ap, Atm[:, b, cols], r=[Atm, c["ones1"]], w=[p3])
            kb.stt(ncar[d].ap, p3[:, 0:8], -1.0, ncar[d].ap, ALU.mult, ALU.add, r=[p3, ncar[d]], w=[ncar[d]])
    kb.P.strict = False
    for d in range(2):
        kb.st(g["CSD"][d * 8:(d + 1) * 8, :], CS[d].ap, r=[CS[d]])
    kb.barrier()
    kb.off = m1
    msk = kb.sb([128, 8, 512], F32, "msk")
    kb.ld(msk.ap, g["ssd_mask"], w=[msk])
    BTs = [kb.sb([128, T], BF16, "BT%d" % i) for i in range(2)]
    CTs = [kb.sb([128, T], BF16, "CT%d" % i) for i in range(2)]
    for gi in range(2):
        kb.ld(BTs[gi].ap, g["XBC"][gi * 128:(gi + 1) * 128, :], w=[BTs[gi]])
        kb.ld(CTs[gi].ap, g["XBC"][256 + gi * 128:256 + (gi + 1) * 128, :], w=[CTs[gi]])
    xsr = kb.rot(2, [128, NB, 64])
    xdr = kb.rot(2, [128, NB, 64], BF16)
    csr = kb.rot(3, [128, 512])
    argr = kb.rot(2, [128, 512])
    Lr = kb.rot(3, [128, 512])
    Mr = kb.rot(3, [128, 512], BF16)
    yr = kb.rot(2, [64, 512])
    pSr = Rot(kb.ps[0:3])
    pOr = Rot(kb.ps[4:6])
    qtiles = TILES
    for d in range(2):
        for h in range(8):
            gi = h // 4
            col = d * 8 + h
            xs = xsr.next()
            kb.ld(xs.ap, g["XSTM"][:, h * 64:(h + 1) * 64].rearrange("(b p) e -> p b e", p=128), w=[xs])
            xd = xdr.next()
            for b in range(NB):
                kb.ts(xd[:, b, :], xs[:, b, :], DTt[:, b, col:col + 1], None, ALU.mult, r=[xs, DTt], w=[xd], eng=("dve", "pool")[b % 2])
            for (t0, n, isctx) in qtiles:
                cr = csr.next()
                kb.ld(cr[:, :n], g["CSD"][col, t0:t0 + n].partition_broadcast(128), w=[cr])
                blocks = []
                q0b = t0 // 128
                nqb = n // 128
                if d == 0:
                    blocks += [(b, None) for b in range(0, q0b)]
                    blocks += [(q0b + i, i) for i in range(nqb)]
                else:
                    blocks += [(q0b + i, i) for i in range(nqb)]
                    if not isctx:
                        blocks += [(b, None) for b in range(q0b + nqb, NB)]
                        blocks += [(0, None), (1, None)]
                pO = pOr.next()
                for bi, (b, dg) in enumerate(blocks):
                    pS = pSr.next()
                    kb.mm(pS[:, :n], BTs[gi][:, b * 128:(b + 1) * 128], CTs[gi][:, t0:t0 + n], r=[BTs[gi], CTs[gi]], w=[pS])
                    L = Lr.next()
                    if dg is None:
                        kb.act(L[:, :n], cr[:, :n], AF.Exp, r=[cr, NCS], w=[L], bias=NCS[:, b, col:col + 1], scale=1.0)
                    else:
                        ag = argr.next()
                        kb.stt(ag[:, :n], cr[:, :n], NCS[:, b, col:col + 1], msk[:, d * 4 + dg, :n], ALU.add, ALU.add,
                               r=[cr, NCS, msk], w=[ag], eng="pool")
                        kb.act(L[:, :n], ag[:, :n], AF.Exp, r=[ag], w=[L])
                    M = Mr.next()
                    kb.tt(M[:, :n], pS[:, :n], L[:, :n], ALU.mult, r=[pS, L], w=[M])
                    kb.mm(pO[:64, :n], xd[:, b, :], M[:, :n], start=bi == 0, stop=bi == len(blocks) - 1, r=[xd, M], w=[pO])
                y = yr.next()
                kb.cp(y[:, :n], pO[:64, :n], r=[pO], w=[y], eng="act")
                kb.st(g["YD"][d][h * 64:(h + 1) * 64, t0:t0 + n], y[:, :n], r=[y])
    kb.barrier()
    kb.off = m0
    Dv = kb.sb([128, 4])
    gn = kb.sb([128, 4])
    kb.ld(Dv.ap, g["ssd_dvec"][l], w=[Dv])
    kb.ld(gn.ap, g["ssd_norm"][l], w=[gn])
    on512 = kb.sb([128, 128])
    kb.ms(on512.ap, 1.0 / 512.0, w=[on512])
    y0r = kb.rot(2, [128, 4, 512])
    y1r = kb.rot(2, [128, 4, 512])
    xr = kb.rot(2, [128, 4, 512])
    zr = kb.rot(2, [128, 4, 512])
    sqs = [kb.sb([128, 512]) for _ in range(4)]
    rstd = kb.sb([128, 512])
    obr = kb.rot(2, [128, 4, 512], BF16)
    pS = kb.ps[7]
    for (t0, n, isctx) in TILES:
        if isctx and not need_ctx:
            continue
        y0 = y0r.next()
        y1 = y1r.next()
        xx = xr.next()
        zz = zr.next()
        kb.ld(y0[:, :, :n], fm(g["YD"][0])[:, :, t0:t0 + n], w=[y0])
        kb.ld(y1[:, :, :n], fm(g["YD"][1])[:, :, t0:t0 + n], w=[y1])
        kb.ld(xx[:, :, :n], fm(g["XSF"])[:, :, t0:t0 + n], w=[xx])
        kb.ld(zz[:, :, :n], fm(g["ZS"])[:, :, t0:t0 + n], w=[zz])
        for kc in range(4):
            kb.tt(y0[:, kc, :n], y0[:, kc, :n], y1[:, kc, :n], ALU.add, r=[y0, y1], w=[y0], eng="pool")
            kb.stt(y0[:, kc, :n], xx[:, kc, :n], Dv[:, kc:kc + 1], y0[:, kc, :n], ALU.mult, ALU.add, r=[xx, Dv, y0], w=[y0])
            kb.tt(y0[:, kc, :n], y0[:, kc, :n], zz[:, kc, :n], ALU.mult, r=[y0, zz], w=[y0], eng="pool")
            kb.act(sqs[kc][:, :n], y0[:, kc, :n], AF.Square, r=[y0], w=[sqs[kc]])
        for kc in range(4):
            kb.mm(pS[:, :n], on512.ap, sqs[kc][:, :n], start=kc == 0, stop=kc == 3, r=[on512, sqs[kc]], w=[pS])
        kb.act(rstd[:, :n], pS[:, :n], AF.Sqrt, r=[pS, c["eps"]], w=[rstd], bias=c["eps"][:, 0:1], scale=1.0)
        kb.recip(rstd[:, :n], rstd[:, :n], r=[rstd], w=[rstd])
        ob = obr.next()
        for kc in range(4):
            kb.tt(sqs[kc][:, :n], y0[:, kc, :n], rstd[:, :n], ALU.mult, r=[y0, rstd], w=[sqs[kc]])
            kb.act(ob[:, kc, :n], sqs[kc][:, :n], AF.Copy, r=[sqs[kc], gn], w=[ob], scale=gn[:, kc:kc + 1])
        kb.st(fm(g["SSDO"])[:, :, t0:t0 + n], ob[:, :, :n], r=[ob])
    kb.off = m0


NF = 16384


def hy_sin(kb, out, ps_in, b_ap, f_ap, m, n, tmp, ki, kf, keys_r):
    kb.ts(tmp[:m, :n], ps_in, b_ap, f_ap, ALU.add, ALU.mult, r=keys_r, w=[tmp])
    kb.ts(tmp[:m, :n], tmp[:m, :n], 1.0 / (2 * PI), 64.5, ALU.mult, ALU.add, r=[tmp], w=[tmp])
    kb.cp(ki[:m, :n], tmp[:m, :n], r=[tmp], w=[ki])
    kb.cp(kf[:m, :n], ki[:m, :n], r=[ki], w=[kf])
    kb.tt(tmp[:m, :n], tmp[:m, :n], kf[:m, :n], ALU.subtract, r=[tmp, kf], w=[tmp])
    kb.stt(tmp[:m, :n], tmp[:m, :n], 0.5, tmp[:m, :n], ALU.is_gt, ALU.subtract, r=[tmp], w=[tmp])
    kb.act(out[:m, :n], tmp[:m, :n], AF.Sin, r=[tmp], w=[out], scale=2 * PI)


def fft_fwd(kb, F, Xm, K, tw, pA, pX, Bt):
    Fr, Fi, nFi = F
    for cc in range(4):
        kb.mm(pA[0][:, cc * 128:(cc + 1) * 128], Xm[:K, cc, :], Fr[:K, :], r=[Xm, Fr], w=[pA[0]])
        kb.mm(pA[1][:, cc * 128:(cc + 1) * 128], Xm[:K, cc, :], Fi[:K, :], r=[Xm, Fi], w=[pA[1]])
    t1, t2, Br, Bi = Bt
    twr, twi = tw
    kb.tt(t1.ap, pA[0].ap, twr.ap, ALU.mult, r=[pA[0], twr], w=[t1])
    kb.tt(t2.ap, pA[1].ap, twi.ap, ALU.mult, r=[pA[1], twi], w=[t2])
    kb.tt(Br.ap, t1.ap, t2.ap, ALU.subtract, r=[t1, t2], w=[Br], eng="pool")
    kb.tt(t1.ap, pA[0].ap, twi.ap, ALU.mult, r=[pA[0], twi], w=[t1])
    kb.tt(t2.ap, pA[1].ap, twr.ap, ALU.mult, r=[pA[1], twr], w=[t2])
    kb.tt(Bi.ap, t1.ap, t2.ap, ALU.add, r=[t1, t2], w=[Bi], eng="pool")
    kb.mm(pX[0].ap, Fr.ap, Br.ap, start=True, stop=False, r=[Fr, Br], w=[pX[0]])
    kb.mm(pX[0].ap, nFi.ap, Bi.ap, start=False, stop=True, r=[nFi, Bi], w=[pX[0]])
    kb.mm(pX[1].ap, Fi.ap, Br.ap, start=True, stop=False, r=[Fi, Br], w=[pX[1]])
    kb.mm(pX[1].ap, Fr.ap, Bi.ap, start=False, stop=True, r=[Fr, Bi], w=[pX[1]])


def phase_hyena(kb, g, l, need_ctx):
    c = g["c"]
    kb.barrier()
    m0 = kb.off
    cw = kb.sb([128, 12, 3])
    cb = kb.sb([128, 12])
    kb.ld(cw.ap, g["hy_conv_w"][l], w=[cw])
    kb.ld(cb.ap, g["hy_conv_b"][l], w=[cb])
    ubr = kb.rot(3, [128, 514])
    accr = kb.rot(3, [128, 512])
    for (t0, n, isctx) in TILES:
        lo, hi = seq_bounds(t0, n)
        a0 = max(lo, t0 - 1)
        a1 = min(hi, t0 + n + 1)
        for kc in range(12):
            ub = ubr.next()
            if a0 > t0 - 1 or a1 < t0 + n + 1:
                kb.ms(ub[:, 0:n + 2], 0.0, w=[ub], eng="pool")
            kb.ld(ub[:, a0 - (t0 - 1):a1 - (t0 - 1)], g["UH"][kc * 128:(kc + 1) * 128, a0:a1], w=[ub])
            acc = accr.next()
            kb.ts(acc[:, :n], ub[:, 0:n], cw[:, kc, 0:1], cb[:, kc:kc + 1], ALU.mult, ALU.add, r=[ub, cw, cb], w=[acc])
            for j in range(1, 3):
                kb.stt(acc[:, :n], ub[:, j:j + n], cw[:, kc, j:j + 1], acc[:, :n], ALU.mult, ALU.add, r=[ub, cw, acc], w=[acc])
            kb.st(g["HC"][kc * 128:(kc + 1) * 128, t0:t0 + n], acc[:, :n], r=[acc])
    seqs = [0, 1] if need_ctx else [0]
    kb.barrier()
    kb.off = m0
    w1 = kb.sb([33, 64])
    w2 = kb.sb([64, 64])
    w3 = kb.sb([64, 2048])
    bf = kb.sb([64, 4])
    nd = kb.sb([128, 4])
    kb.ld(w1.ap, g["hy_w1"][l], w=[w1])
    kb.ld(w2.ap, g["hy_w2"][l], w=[w2])
    kb.ld(w3.ap, g["hy_w3"][l], w=[w3])
    kb.ld(bf.ap, g["hy_bf"][l], w=[bf])
    kb.ld(nd.ap, g["hy_negdelta"], w=[nd])
    ftr = kb.rot(2, [33, 512])
    tur = kb.rot(2, [128, 512])
    tmp = kb.sb([64, 512])
    ki = kb.sb([64, 512], I32)
    kf = kb.sb([64, 512])
    h1 = kb.sb([64, 512])
    h2r = kb.rot(2, [64, 512])
    decr = kb.rot(2, [128, 512])
    gor = kb.rot(3, [128, 512])
    zt = kb.sb([128, 512])
    kb.ms(zt.ap, 0.0, w=[zt])
    p1 = kb.ps[0]
    p2 = kb.ps[1]
    p0 = kb.ps[2]
    phr = Rot(kb.ps[3:7])
    for sq in seqs:
        for mt in range(NF // 512):
            if sq == 0:
                dirn = 0 if mt < 16 else 1
            else:
                dirn = 0 if mt == 0 else (1 if mt == NF // 512 - 1 else None)
            if dirn is None:
                for oc in range(8):
                    kb.st(g["GF"][sq][oc * 128:(oc + 1) * 128, mt * 512:(mt + 1) * 512], zt.ap, r=[zt])
                continue
            ft = ftr.next()
            tu = tur.next()
            kb.ld(ft.ap, g["hy_feats"][sq, :, mt * 512:(mt + 1) * 512], w=[ft])
            kb.ld(tu.ap, g["hy_tunit"][sq, mt * 512:(mt + 1) * 512].partition_broadcast(128), w=[tu])
            kb.mm(p1[:64, :], w1.ap, ft.ap, r=[w1, ft], w=[p1])
            hy_sin(kb, h1, p1[:64, :], bf[:, 0:1], bf[:, 1:2], 64, 512, tmp, ki, kf, [p1, bf])
            kb.mm(p2[:64, :], w2.ap, h1.ap, r=[w2, h1], w=[p2])
            h2 = h2r.next()
            hy_sin(kb, h2, p2[:64, :], bf[:, 2:3], bf[:, 3:4], 64, 512, tmp, ki, kf, [p2, bf])
            for cc in range(4):
                dec = decr.next()
                kb.act(dec.ap, tu.ap, AF.Exp, r=[tu, nd], w=[dec], scale=nd[:, cc:cc + 1])
                for o in range(2):
                    ph = phr.next()
                    col = dirn * 1024 + o * 512 + cc * 128
                    kb.mm(ph.ap, w3[:, col:col + 128], h2.ap, r=[w3, h2], w=[ph])
                    go = gor.next()
                    kb.tt(go.ap, ph.ap, dec.ap, ALU.mult, r=[ph, dec], w=[go])
                    if mt == 0:
                        colb = 1024 + o * 512 + cc * 128
                        kb.mm(p0[:, 0:1], w3[:, colb:colb + 128], h2[:, 0:1], r=[w3, h2], w=[p0])
                        kb.P.strict = True
                        kb.tt(go[:, 0:1], go[:, 0:1], p0[:, 0:1], ALU.add, r=[go, p0], w=[go])
                        kb.P.strict = False
                    oc = o * 4 + cc
                    kb.st(g["GF"][sq][oc * 128:(oc + 1) * 128, mt * 512:(mt + 1) * 512], go.ap, r=[go])
    kb.barrier()
    kb.off = m0
    grow = kb.rot(2, [128, NF])
    sqt = kb.sb([128, 4096])
    ssum = kb.sb([128, 4])
    rs_ = kb.sb([128, 1])
    for sq in seqs:
        for oc in range(8):
            gr = grow.next()
            kb.ld(gr.ap, g["GF"][sq][oc * 128:(oc + 1) * 128, :], w=[gr])
            kb.P.strict = True
            for q in range(4):
                kb.act(sqt.ap, gr[:, q * 4096:(q + 1) * 4096], AF.Square, r=[gr], w=[sqt])
                kb.P.add("dve", lambda e, q=q: e.reduce_sum(out=ssum[:, q:q + 1], in_=sqt.ap, axis=AX.X), [sqt.key], [ssum.key])
            kb.P.add("dve", lambda e: e.reduce_sum(out=rs_.ap, in_=ssum.ap, axis=AX.X), [ssum.key], [rs_.key])
            kb.act(rs_.ap, rs_.ap, AF.Sqrt, r=[rs_, c["eps"]], w=[rs_], bias=c["eps"][:, 0:1], scale=1.0)
            kb.recip(rs_.ap, rs_.ap, r=[rs_], w=[rs_])
            kb.P.strict = False
            for q in range(4):
                kb.ts(gr[:, q * 4096:(q + 1) * 4096], gr[:, q * 4096:(q + 1) * 4096], rs_[:, 0:1], None, ALU.mult,
                      r=[gr, rs_], w=[gr], eng=("dve", "pool")[q % 2])
            kb.st(g["GF"][sq][oc * 128:(oc + 1) * 128, :], gr.ap, r=[gr])
    kb.barrier()
    kb.off = m0
    Fr = kb.sb([128, 128], F32, "Fr")
    Fi = kb.sb([128, 128], F32, "Fi")
    nFi = kb.sb([128, 128], F32, "nFi")
    twr = kb.sb([128, 512], F32, "twr")
    twi = kb.sb([128, 512], F32, "twi")
    kb.ld(Fr.ap, g["dftr"], w=[Fr])
    kb.ld(Fi.ap, g["dfti"], w=[Fi])
    kb.ts(nFi.ap, Fi.ap, -1.0, None, ALU.mult, r=[Fi], w=[nFi])
    kb.ld(twr.ap, g["twr"], w=[twr])
    kb.ld(twi.ap, g["twi"], w=[twi])
    F = (Fr, Fi, nFi)
    tw = (twr, twi)
    Bt = (kb.sb([128, 512]), kb.sb([128, 512]), kb.sb([128, 512]), kb.sb([128, 512]))
    pA = kb.ps[0:2]
    pX = kb.ps[2:4]
    m2 = kb.off
    xmr = kb.rot(2, [128, 4, 128])
    xer = kb.rot(2, [128, 512])
    xir = kb.rot(2, [128, 512])
    for sq in seqs:
        for gi in range(256):
            xm = xmr.next()
            kb.ld(xm.ap, g["GF"][sq][gi * 4:(gi + 1) * 4, :].rearrange("c (a b) -> a c b", b=128), w=[xm])
            fft_fwd(kb, F, xm, 128, tw, pA, pX, Bt)
            xe = xer.next()
            xi = xir.next()
            kb.cp(xe.ap, pX[0].ap, r=[pX[0]], w=[xe], eng="act")
            kb.cp(xi.ap, pX[1].ap, r=[pX[1]], w=[xi], eng="act")
            kb.st(g["GH"][sq][0, :, gi * 512:(gi + 1) * 512], xe.ap, r=[xe])
            kb.st(g["GH"][sq][1, :, gi * 512:(gi + 1) * 512], xi.ap, r=[xi])
    kb.barrier()
    kb.off = m2
    bcol = kb.sb([64, 2, 512])
    kb.ld(bcol.ap, g["hy_bias"][l, 0].partition_broadcast(64).rearrange("p (o c) -> p o c", o=2), w=[bcol])
    zr = kb.rot(2, [64, 4, 128])
    g1r = kb.rot(2, [64, 2, 4, 128])
    ghr = kb.rot(2, [128, 2, 512])
    Yr_ = kb.sb([128, 4, 128])
    Yi_ = kb.sb([128, 4, 128])
    t1 = Bt[0]
    t2 = Bt[1]
    Dr_ = Bt[2]
    Di_ = Bt[3]
    ysc = kb.sb([64, 4, 128])
    znr = kb.rot(2, [64, 4, 128])
    zor = kb.rot(2, [64, 4, 128], BF16)
    pC = kb.ps[4:6]
    pY = kb.ps[6]
    for sq in seqs:
        K = 64 if sq == 0 else 2
        toff = CT if sq == 0 else 0
        for gi in range(128):
            c0 = gi * 4
            z = zr.next()
            gt = g1r.next()

            def tb(rows):
                return g["HC"][rows:rows + 4, toff:toff + K * 128].rearrange("c (a b) -> a c b", b=128)
            kb.ld(z[:K], tb(c0), w=[z])
            kb.ld(gt[:K, 0], tb(512 + c0), w=[gt])
            kb.ld(gt[:K, 1], tb(1024 + c0), w=[gt])
            for o in range(2):
                fft_fwd(kb, F, z, K, tw, pA, pX, Bt)
                gh = ghr.next()
                oc0 = o * 512 + c0
                kb.ld(gh[:, 0, :], g["GH"][sq][0, :, oc0 * 128:(oc0 + 4) * 128], w=[gh])
                kb.ld(gh[:, 1, :], g["GH"][sq][1, :, oc0 * 128:(oc0 + 4) * 128], w=[gh])
                Yrf = Yr_.ap.rearrange("p a b -> p (a b)")
                Yif = Yi_.ap.rearrange("p a b -> p (a b)")
                kb.tt(t1.ap, pX[0].ap, gh[:, 0, :], ALU.mult, r=[pX[0], gh], w=[t1])
                kb.tt(t2.ap, pX[1].ap, gh[:, 1, :], ALU.mult, r=[pX[1], gh], w=[t2])
                kb.tt(Yrf, t1.ap, t2.ap, ALU.subtract, r=[t1, t2], w=[Yr_], eng="pool")
                kb.tt(t1.ap, pX[0].ap, gh[:, 1, :], ALU.mult, r=[pX[0], gh], w=[t1])
                kb.tt(t2.ap, pX[1].ap, gh[:, 0, :], ALU.mult, r=[pX[1], gh], w=[t2])
                kb.tt(Yif, t1.ap, t2.ap, ALU.add, r=[t1, t2], w=[Yi_], eng="pool")
                for cc in range(4):
                    sl = slice(cc * 128, (cc + 1) * 128)
                    kb.mm(pC[0][:, sl], Yr_[:, cc, :], Fr.ap, start=True, stop=False, r=[Yr_, Fr], w=[pC[0]])
                    kb.mm(pC[0][:, sl], Yi_[:, cc, :], Fi.ap, start=False, stop=True, r=[Yi_, Fi], w=[pC[0]])
                    kb.mm(pC[1][:, sl], Yi_[:, cc, :], Fr.ap, start=True, stop=False, r=[Yi_, Fr], w=[pC[1]])
                    kb.mm(pC[1][:, sl], Yr_[:, cc, :], nFi.ap, start=False, stop=True, r=[Yr_, nFi], w=[pC[1]])
                kb.tt(t1.ap, pC[0].ap, twr.ap, ALU.mult, r=[pC[0], twr], w=[t1])
                kb.tt(t2.ap, pC[1].ap, twi.ap, ALU.mult, r=[pC[1], twi], w=[t2])
                kb.tt(Dr_.ap, t1.ap, t2.ap, ALU.add, r=[t1, t2], w=[Dr_], eng="pool")
                kb.tt(t1.ap, pC[1].ap, twr.ap, ALU.mult, r=[pC[1], twr], w=[t1])
                kb.tt(t2.ap, pC[0].ap, twi.ap, ALU.mult, r=[pC[0], twi], w=[t2])
                kb.tt(Di_.ap, t1.ap, t2.ap, ALU.subtract, r=[t1, t2], w=[Di_], eng="pool")
                kb.mm(pY[:K, :], Fr[:, 0:K], Dr_.ap, start=True, stop=False, r=[Fr, Dr_], w=[pY])
                kb.mm(pY[:K, :], Fi[:, 0:K], Di_.ap, start=False, stop=True, r=[Fi, Di_], w=[pY])
                kb.act(ysc[:K].rearrange("p a b -> p (a b)"), pY[:K, :], AF.Copy, r=[pY], w=[ysc], scale=1.0 / NF)
                zn = znr.next()
                for cc in range(4):
                    kb.stt(zn[:K, cc, :], z[:K, cc, :], bcol[:K, o, c0 + cc:c0 + cc + 1], ysc[:K, cc, :], ALU.mult, ALU.add,
                           r=[z, bcol, ysc], w=[zn])
                if o == 0:
                    kb.tt(zn[:K], zn[:K], gt[:K, 0], ALU.mult, r=[zn, gt], w=[zn], eng="pool")
                    z = zn
                else:
                    zo = zor.next()
                    kb.tt(zo[:K], zn[:K], gt[:K, 1], ALU.mult, r=[zn, gt], w=[zo], eng="pool")
                    kb.st(g["HYO"][c0:c0 + 4, toff:toff + K * 128].rearrange("c (a b) -> a c b", b=128), zo[:K], r=[zo])
    kb.off = m0


NLP = 1
W_SHAPES = {
    "mod_w": (NLP, 1024, 6144), "mod_bT": (NLP, 128, 48), "gains": (NLP, 128, 4, 8),
    "w_in": (NLP, 1024, IN_COLS), "ssd_dt_bias": (NLP, 1, 16),
    "ssd_conv_w": (NLP, 128, 8, 5), "ssd_conv_b": (NLP, 128, 8), "ssd_a_log": (NLP, 1, 16), "ssd_dvec": (NLP, 128, 4),
    "ssd_norm": (NLP, 128, 4),
    "hy_conv_w": (NLP, 128, 12, 3), "hy_conv_b": (NLP, 128, 12), "hy_w1": (NLP, 33, 64), "hy_w2": (NLP, 64, 64),
    "hy_w3": (NLP, 64, 2048), "hy_bf": (NLP, 64, 4), "hy_bias": (NLP, 1, 1024),
    "da_lambda": (NLP, 1, 256), "da_subln": (NLP, 128, 1),
    "w_br_da": (NLP, 512, 1024), "w_br_ssd": (NLP, 512, 1024), "w_br_mla": (NLP, 512, 1024), "w_br_hy": (NLP, 512, 1024),
    "w_out": (NLP, 1024, 1024), "ffn_w1": (NLP, 1024, FFH), "ffn_w3": (NLP, 1024, FFH), "ffn_w2": (NLP, FFH, 1024),
    "mla_w_uq": (NLP, 384, 768), "mla_w_ukv": (NLP, 256, 1024), "mla_q_norm": (NLP, 128, 3), "mla_kv_norm": (NLP, 128, 2),
}
C_SHAPES = {
    "hy_negdelta": (128, 4), "hy_feats": (2, 33, NF), "hy_tunit": (2, NF), "dftr": (128, 128), "dfti": (128, 128),
    "twr": (128, 512), "twi": (128, 512),
    "ident": (128, 128), "rrot": (128, 128), "tri": (128, 2, 128), "ssd_mask": (128, 8, 512), "ropeC": (128, S), "ropeS": (128, S),
}


def build(dbg=(), stop_after=None):
    import os
    PH_SKIP = set(os.environ.get("K_PHSKIP", "").split(","))
    kb = KB(dbg)
    g = {}
    g["x0T"] = kb.inp("x0T", (D, T))
    g["cT"] = kb.inp("cT", (128, 16))
    g["lamc"] = kb.inp("lamc", (128, 2))
    for k, shp in list(W_SHAPES.items()) + list(C_SHAPES.items()):
        g[k] = kb.inp(k, shp)
    for nm, rows, dt in [("H", 1024, BF16), ("QK", 1024, BF16), ("ZS", 512, F32), ("US", 1024, F32),
                         ("CQ", 384, F32), ("CKV", 256, F32), ("KR", 64, BF16), ("UH", 1536, F32)]:
        g[nm] = kb.dram(nm, (rows, T), dt)
    for nm, rows, dt in [("DAO", 512, BF16), ("MLAO", 512, BF16), ("SSDO", 512, BF16), ("HYO", 512, BF16),
                         ("QN", 512, BF16), ("QR", 256, BF16), ("KN", 512, BF16)]:
        g[nm] = kb.dram(nm, (rows, T), dt)
    g["VM"] = kb.dram("VM", (T, 512), BF16)
    g["HC"] = kb.dram("HC", (1536, T), F32)
    g["GF"] = [kb.dram("GF0", (1024, NF), F32), kb.dram("GF1", (1024, NF), F32)]
    g["GH"] = [kb.dram("GH0", (2, 128, 1024 * 128), F32), kb.dram("GH1", (2, 128, 1024 * 128), F32)]
    g["XSF"] = kb.dram("XSF", (512, T), F32)
    g["XSTM"] = kb.dram("XSTM", (T, 512), F32)
    g["XBC"] = kb.dram("XBC", (512, T), BF16)
    g["CSD"] = kb.dram("CSD", (16, T), F32)
    g["YD"] = [kb.dram("YD0", (512, T), F32), kb.dram("YD1", (512, T), F32)]
    g["VDA"] = kb.dram("VDA", (T, 512), BF16)
    g["DT"] = kb.dram("DT", (T, 16), F32)
    g["X"] = [g["x0T"], kb.dram("X1", (D, T)), kb.dram("xoutT", (D, T), out=True)]
    setup_consts(kb, g)
    for l in range(NLP):
        Xin = g["X"][0]
        phase_mod(kb, g, l)
        if stop_after == ("mod", l):
            break
        phase_inproj(kb, g, l, Xin)
        if stop_after == ("inproj", l):
            break
        need_ctx = True
        if "da" not in PH_SKIP:
            phase_da(kb, g, l, need_ctx)
        if stop_after == ("da", l):
            break
        if "mla" not in PH_SKIP:
            phase_mla(kb, g, l, need_ctx)
        if stop_after == ("mla", l):
            break
        if "ssd" not in PH_SKIP:
            phase_ssd(kb, g, l, need_ctx)
        if stop_after == ("ssd", l):
            break
        if "hy" not in PH_SKIP:
            phase_hyena(kb, g, l, need_ctx)
        if stop_after == ("hy", l):
            break
        phase_merge(kb, g, l, Xin, g["X"][1], need_ctx)
        if stop_after == ("merge", l):
            break
        phase_ffn(kb, g, l, g["X"][1], g["X"][2], need_ctx, Yout=None)
    kb.P.emit()
    return kb


def _rope_tables():
    t = np.arange(S)
    pos_row = (t // 64).astype(np.float32)
    pos_col = (t % 64).astype(np.float32)
    inv = (10000.0 ** (-np.arange(16, dtype=np.float32) / 16)).astype(np.float32)
    ang = np.concatenate([pos_row[:, None] * inv, pos_col[:, None] * inv], axis=-1).astype(np.float32)
    cos = np.cos(ang).astype(np.float32).T
    sin = np.sin(ang).astype(np.float32).T
    return np.ascontiguousarray(np.tile(cos, (4, 1))), np.ascontiguousarray(np.tile(sin, (4, 1)))


def host_consts():
    c = {}
    c["ident"] = np.eye(128, dtype=np.float32)
    R = np.zeros((128, 128), np.float32)
    for m in range(128):
        if m % 64 < 32:
            R[m, m + 32] = -1.0
        else:
            R[m, m - 32] = 1.0
    c["rrot"] = np.ascontiguousarray(R.T)
    c["ropeC"], c["ropeS"] = _rope_tables()
    deltas = np.abs(np.linspace(math.log(1e-2) / 1.5, math.log(1e-2) / 0.3, 512, dtype=np.float32)).astype(np.float32)
    c["hy_negdelta"] = np.ascontiguousarray((-deltas).reshape(4, 128).T)
    feats = np.zeros((2, 33, NF), np.float32)
    tun = np.full((2, NF), 1e9, np.float32)
    band_f = np.linspace(1e-4, 15, 16, dtype=np.float32)
    for sq, n in enumerate([S, CT]):
        m = np.arange(NF)
        lag = np.where(m < n, m, NF - m)
        valid = (m < n) | (m > NF - n)
        lagf = lag.astype(np.float32)
        t_unit = (lagf / np.float32(n - 1)).astype(np.float32)
        wv = (np.float32(2.0 * math.pi) * lagf / np.float32(n)).astype(np.float32)
        f = np.concatenate([t_unit[None, :], np.cos(wv[None, :] * band_f[:, None]), -np.sin(wv[None, :] * band_f[:, None])], axis=0)
        feats[sq] = np.where(valid[None, :], f, 0.0).astype(np.float32)
        tun[sq] = np.where(valid, t_unit, 1e9).astype(np.float32)
    c["hy_feats"] = feats
    c["hy_tunit"] = tun
    jk = np.outer(np.arange(128), np.arange(128)).astype(np.float64)
    c["dftr"] = np.cos(2 * np.pi * jk / 128).astype(np.float32)
    c["dfti"] = (-np.sin(2 * np.pi * jk / 128)).astype(np.float32)
    c["twr"] = np.ascontiguousarray(np.tile(np.cos(2 * np.pi * jk / NF), (1, 4)).astype(np.float32))
    c["twi"] = np.ascontiguousarray(np.tile(-np.sin(2 * np.pi * jk / NF), (1, 4)).astype(np.float32))
    si = np.arange(128)[:, None]
    ti = np.arange(128)[None, :]
    c["tri"] = np.ascontiguousarray(np.stack([(si <= ti), (si >= ti)], axis=1).astype(np.float32))
    tq = np.arange(512)[None, :]
    mk = np.zeros((128, 8, 512), np.float32)
    for dg in range(4):
        mk[:, dg, :] = np.where(tq - 128 * dg >= si, 0.0, -1e30)
        mk[:, 4 + dg, :] = np.where(tq - 128 * dg <= si, 0.0, -1e30)
    c["ssd_mask"] = mk
    return c


def host_weights(inp):
    w = {}
    w["mod_w"] = np.ascontiguousarray(inp["mod_w"], np.float32)
    w["mod_bT"] = np.ascontiguousarray(inp["mod_b"].reshape(NL, 48, 128).transpose(0, 2, 1))
    gains = np.stack([inp["norm_mix_pre"], inp["norm_mix_post"], inp["norm_ffn_pre"], inp["norm_ffn_post"]], axis=1)
    w["gains"] = np.ascontiguousarray(gains.reshape(NL, 4, 8, 128).transpose(0, 3, 1, 2))
    w["w_in"] = np.ascontiguousarray(inp["w_in"], np.float32)
    w["ssd_dt_bias"] = np.ascontiguousarray(inp["ssd_dt_bias"].reshape(NL, 1, 16))
    w["da_lambda"] = np.ascontiguousarray(inp["da_lambda"].reshape(NL, 1, 256))
    w["hy_conv_w"] = np.ascontiguousarray(inp["hy_conv_w"].transpose(0, 2, 1).reshape(NL, 12, 128, 3).transpose(0, 2, 1, 3))
    w["hy_conv_b"] = np.ascontiguousarray(inp["hy_conv_b"].reshape(NL, 12, 128).transpose(0, 2, 1))
    w["hy_w1"] = np.ascontiguousarray(inp["hy_w1"])
    w["hy_w2"] = np.ascontiguousarray(inp["hy_w2"])
    w["hy_w3"] = np.ascontiguousarray(inp["hy_w3"])
    w["hy_bf"] = np.ascontiguousarray(np.stack([inp["hy_b1"], inp["hy_freq1"], inp["hy_b2"], inp["hy_freq2"]], axis=-1))
    w["hy_bias"] = np.ascontiguousarray(inp["hy_bias"].reshape(NL, 1, 1024))
    w["ssd_conv_w"] = np.ascontiguousarray(inp["ssd_conv_w"].transpose(0, 2, 1).reshape(NL, 8, 128, 5).transpose(0, 2, 1, 3))
    w["ssd_conv_b"] = np.ascontiguousarray(inp["ssd_conv_b"].reshape(NL, 8, 128).transpose(0, 2, 1))
    w["ssd_a_log"] = np.ascontiguousarray(inp["ssd_a_log"].reshape(NL, 1, 16))
    w["ssd_dvec"] = np.ascontiguousarray(np.repeat(inp["ssd_d"], 64, axis=1).reshape(NL, 4, 128).transpose(0, 2, 1))
    w["ssd_norm"] = np.ascontiguousarray(inp["ssd_norm"].reshape(NL, 4, 128).transpose(0, 2, 1))
    w["da_subln"] = np.ascontiguousarray(inp["da_subln"].reshape(NL, 128, 1))
    for k_ in ["w_br_da", "w_br_ssd", "w_br_mla", "w_br_hy", "w_out", "ffn_w1", "ffn_w3", "ffn_w2"]:
        w[k_] = np.ascontiguousarray(inp[k_], np.float32)
    w["mla_w_uq"] = np.ascontiguousarray(inp["mla_w_uq"])
    w["mla_w_ukv"] = np.ascontiguousarray(inp["mla_w_ukv"])
    w["mla_q_norm"] = np.ascontiguousarray(inp["mla_q_norm"].reshape(NL, 3, 128).transpose(0, 2, 1))
    w["mla_kv_norm"] = np.ascontiguousarray(inp["mla_kv_norm"].reshape(NL, 2, 128).transpose(0, 2, 1))
    return w


def host_core_inputs(inp, b):
    d = {}
    xall = np.concatenate([inp["ctx"][b], inp["x"][b]], axis=0)
    d["x0T"] = np.ascontiguousarray(xall.T)
    cc = np.stack([inp["c"][b], inp["c_ctx"]], axis=-1)
    d["cT"] = np.ascontiguousarray(cc.reshape(8, 128, 2).transpose(1, 0, 2).reshape(128, 16))
    return d


_CACHE = {}


def kernel(**inputs):
    inp = {k: np.asarray(v) for k, v in inputs.items()}
    if "kb" not in _CACHE:
        _CACHE["kb"] = build()
        _CACHE["consts"] = host_consts()
    kb = _CACHE["kb"]
    wfull = host_weights(inp)
    cores = [host_core_inputs(inp, b) for b in range(4)]
    xcur = [cores[b]["x0T"] for b in range(4)]
    for l in range(NL):
        base = dict(_CACHE["consts"])
        for k_, v_ in wfull.items():
            base[k_] = np.ascontiguousarray(v_[l:l + 1])
        lam_init = 0.8 - 0.6 * math.exp(-0.3 * l)
        lamc = np.empty((128, 2), np.float32)
        lamc[:, 0] = -lam_init
        lamc[:, 1] = 1.0 - lam_init
        base["lamc"] = lamc
        maps = []
        for core in range(8):
            m = dict(base)
            m["cT"] = cores[core % 4]["cT"]
            m["x0T"] = xcur[core % 4]
            maps.append(m)
        res = run_bass_kernel_spmd(kb.nc, maps, core_ids=list(range(8)))
        xcur = [np.ascontiguousarray(np.asarray(res.results[b]["xoutT"], dtype=np.float32)) for b in range(4)]
    out = np.empty((4, S, D), np.float32)
    for b in range(4):
        out[b] = xcur[b][:, CT:].T
    return out
```
